# Optimizing a Trainium2 kernel written in Bass

```python
import math
import jax, jax.numpy as jnp
from jax import lax
import numpy as np


D_MODEL = 1024
BATCH = 16
SEQ = 2048
DEPTH = 1

ATTN_HEADS = 8
HEAD_DIM = 64
ATTN_WIDTH = ATTN_HEADS * HEAD_DIM
CONV_GROUPS = 8
CONV_CH = D_MODEL - ATTN_WIDTH
MIX_WIDTH = ATTN_WIDTH + CONV_CH
IN_COLS = 3 * ATTN_WIDTH + 2 * CONV_CH
CONV_KERNEL = 31
MOBA_BLOCK = 256
MOBA_TOPK = 3
Q_CHUNK = 128
NUM_BUCKETS = 32
MAX_DISTANCE = 128
D_FF = -(-8 * D_MODEL // (3 * 256)) * 256
RMS_EPS = 1e-6
LN_EPS = 1e-5
NEG_INF = -1e30
ATTN_SCALE = HEAD_DIM ** -0.5

kernel_name = 'hymba_moba_conformer_block'


def rms_norm(x, g):
    xf = x.astype(jnp.float32)
    y = xf * lax.rsqrt(jnp.mean(xf * xf, axis=-1, keepdims=True) + RMS_EPS)
    return (y * g.astype(jnp.float32)).astype(x.dtype)


def t5_bucket(dist):
    max_exact = NUM_BUCKETS // 2
    d = jnp.maximum(dist, 1).astype(jnp.float32)
    large = max_exact + (jnp.log(d / max_exact) / math.log(MAX_DISTANCE / max_exact)
                         * (NUM_BUCKETS - max_exact)).astype(jnp.int32)
    large = jnp.minimum(large, NUM_BUCKETS - 1)
    return jnp.where(dist < max_exact, dist, large)


def moba_attention(q, k, v, rel_bias):
    B, H, S, Dh = q.shape
    nb = -(-S // MOBA_BLOCK)
    s_pad = nb * MOBA_BLOCK
    n_sel = max(1, min(MOBA_TOPK, nb - 1))
    nq = S // Q_CHUNK
    pad = ((0, 0), (0, 0), (0, s_pad - S), (0, 0))
    k_blocks = jnp.pad(k, pad).reshape(B, H, nb, MOBA_BLOCK, Dh)
    v_blocks = jnp.pad(v, pad).reshape(B, H, nb, MOBA_BLOCK, Dh)

    k_mean = jnp.mean(k_blocks.astype(jnp.float32), axis=3)
    gate = jnp.einsum('bhsd,bhnd->bhsn', q.astype(jnp.float32), k_mean)
    q_block = jnp.arange(S, dtype=jnp.int32) // MOBA_BLOCK
    fully_past = jnp.arange(nb, dtype=jnp.int32)[None, :] < q_block[:, None]
    gate = jnp.where(fully_past, gate, NEG_INF)
    _, sel = lax.top_k(gate, n_sel)
    sel = sel.astype(jnp.int32)

    bias_ht = rel_bias.T.astype(jnp.float32)
    head_ix = jnp.arange(H)
    offs = jnp.arange(MOBA_BLOCK, dtype=jnp.int32)
    rank_ix = jnp.arange(n_sel, dtype=jnp.int32)

    q_c = q.reshape(B, H, nq, Q_CHUNK, Dh).transpose(0, 2, 1, 3, 4).reshape(B * nq, H, Q_CHUNK, Dh)
    sel_c = sel.reshape(B, H, nq, Q_CHUNK, n_sel).transpose(0, 2, 1, 3, 4).reshape(B * nq, H, Q_CHUNK, n_sel)
    b_ids = jnp.repeat(jnp.arange(B, dtype=jnp.int32), nq)
    c_ids = jnp.tile(jnp.arange(nq, dtype=jnp.int32), B)

    def chunk(args):
        qc, sc, b, c = args
        kb = k_blocks[b]
        vb = v_blocks[b]
        t = c * Q_CHUNK + jnp.arange(Q_CHUNK, dtype=jnp.int32)
        own = (c * Q_CHUNK) // MOBA_BLOCK
        k_sel = kb[head_ix[:, None, None], sc]
        v_sel = vb[head_ix[:, None, None], sc]
        s_sel = jnp.einsum('hqd,hqnkd->hqnk', qc, k_sel).astype(jnp.float32) * ATTN_SCALE
        dist_sel = t[None, :, None, None] - (sc[..., None] * MOBA_BLOCK + offs)
        s_sel = s_sel + bias_ht[head_ix[:, None, None, None], t5_bucket(jnp.maximum(dist_sel, 0))]
        valid = (rank_ix < own)[None, None, :, None]
        s_sel = jnp.where(valid, s_sel, NEG_INF)
        k_own = kb[:, own]
        v_own = vb[:, own]
        s_own = jnp.einsum('hqd,hkd->hqk', qc, k_own).astype(jnp.float32) * ATTN_SCALE
        dist_own = t[:, None] - (own * MOBA_BLOCK + offs)[None, :]
        s_own = s_own + bias_ht[:, t5_bucket(jnp.maximum(dist_own, 0))]
        s_own = jnp.where((dist_own >= 0)[None], s_own, NEG_INF)
        logits = jnp.concatenate([s_sel.reshape(H, Q_CHUNK, n_sel * MOBA_BLOCK), s_own], axis=-1)
        p = jax.nn.softmax(logits, axis=-1).astype(v.dtype)
        p_sel = p[..., :n_sel * MOBA_BLOCK].reshape(H, Q_CHUNK, n_sel, MOBA_BLOCK)
        p_own = p[..., n_sel * MOBA_BLOCK:]
        return (jnp.einsum('hqnk,hqnkd->hqd', p_sel, v_sel)
                + jnp.einsum('hqk,hkd->hqd', p_own, v_own))

    out = lax.map(chunk, (q_c, sel_c, b_ids, c_ids))
    return out.reshape(B, nq, H, Q_CHUNK, Dh).transpose(0, 1, 3, 2, 4).reshape(B, S, H * Dh)


def conformer_conv(a, g, w_dw, b_dw, ln_g, ln_b):
    C = a.shape[-1]
    u = a * jax.nn.sigmoid(g)
    u = lax.conv_general_dilated(u, w_dw[:, None, :], window_strides=(1,),
                                 padding=((CONV_KERNEL - 1, 0),),
                                 dimension_numbers=('NWC', 'WIO', 'NWC'),
                                 feature_group_count=C) + b_dw
    uf = u.astype(jnp.float32)
    mu = jnp.mean(uf, axis=-1, keepdims=True)
    var = jnp.mean(jnp.square(uf - mu), axis=-1, keepdims=True)
    un = (uf - mu) * lax.rsqrt(var + LN_EPS) * ln_g.astype(jnp.float32) + ln_b.astype(jnp.float32)
    return jax.nn.silu(un).astype(a.dtype)


def setup_inputs(seed: int = 0) -> dict:
    key = jax.random.key(seed)
    ks = jax.random.split(key, 16)
    f32 = jnp.float32
    nrm = lambda k, shape, scale: jax.random.normal(k, shape, f32) * scale
    return {
        'x': jax.random.normal(ks[0], (BATCH, SEQ, D_MODEL), f32),
        'mix_norm_g': 1.0 + nrm(ks[1], (DEPTH, D_MODEL), 0.02),
        'w_in': nrm(ks[2], (DEPTH, D_MODEL, IN_COLS), D_MODEL ** -0.5),
        'rel_bias': nrm(ks[3], (NUM_BUCKETS, ATTN_HEADS), 0.5),
        'conv_w': nrm(ks[4], (DEPTH, CONV_KERNEL, CONV_CH), CONV_KERNEL ** -0.5),
        'conv_b': nrm(ks[5], (DEPTH, CONV_CH), 0.02),
        'conv_ln_g': 1.0 + nrm(ks[6], (DEPTH, CONV_CH), 0.02),
        'conv_ln_b': nrm(ks[7], (DEPTH, CONV_CH), 0.02),
        'w_out': nrm(ks[8], (DEPTH, MIX_WIDTH, D_MODEL), MIX_WIDTH ** -0.5),
        'ffn_norm_g': 1.0 + nrm(ks[9], (DEPTH, D_MODEL), 0.02),
        'w_gate': nrm(ks[10], (DEPTH, D_MODEL, D_FF), D_MODEL ** -0.5),
        'w_up': nrm(ks[11], (DEPTH, D_MODEL, D_FF), D_MODEL ** -0.5),
        'w_down': nrm(ks[12], (DEPTH, D_FF, D_MODEL), D_FF ** -0.5),
        'final_norm_g': 1.0 + nrm(ks[13], (D_MODEL,), 0.02),
    }


def reference(x, mix_norm_g, w_in, rel_bias, conv_w, conv_b, conv_ln_g, conv_ln_b,
              w_out, ffn_norm_g, w_gate, w_up, w_down, final_norm_g):
    B, S, _ = x.shape
    splits = [ATTN_WIDTH, 2 * ATTN_WIDTH, 3 * ATTN_WIDTH, 3 * ATTN_WIDTH + CONV_CH]

    def heads(t):
        return t.reshape(B, S, ATTN_HEADS, HEAD_DIM).transpose(0, 2, 1, 3)

    h = x
    for l in range(DEPTH):
        u = rms_norm(h, mix_norm_g[l])
        proj = u @ w_in[l]
        q, k, v, glu_a, glu_g = jnp.split(proj, splits, axis=-1)
        attn_out = moba_attention(heads(q), heads(k), heads(v), rel_bias)
        conv_out = conformer_conv(glu_a, glu_g, conv_w[l], conv_b[l],
                                  conv_ln_g[l], conv_ln_b[l])
        mixed = jnp.concatenate([attn_out, conv_out], axis=-1)
        h = h + mixed @ w_out[l]
        u = rms_norm(h, ffn_norm_g[l])
        h = h + (jax.nn.silu(u @ w_gate[l]) * (u @ w_up[l])) @ w_down[l]
    return rms_norm(h, final_norm_g)
```

```python
import math
import numpy as np
import concourse.bass as bass
import concourse.mybir as mybir
from concourse.alu_op_type import AluOpType as ALU
from concourse.bass_utils import run_bass_kernel_spmd

F32 = mybir.dt.float32
BF16 = mybir.dt.bfloat16
AF = mybir.ActivationFunctionType
AX = mybir.AxisListType

D = 1024
T = 4096
SEQ = 2048
G = 512
NG = T // G
DFF = 2816
NFC = DFF // 128
INC = 2560
NEG = -30000.0
RMS_EPS = 1e-6
LN_EPS = 1e-5

C_G1, C_G2, C_G3 = 0, 1024, 2048
C_CW = 3072
C_CB = C_CW + 124
C_LG = C_CB + 4
C_LB = C_LG + 4
C_B31 = C_LB + 4
C_TT = C_B31 + 8
NCST = C_TT + 2048


import os
_STOP_AT = float(os.environ.get("KSTOP", "999"))


class _Stop(Exception):
    pass


def _chk(stage):
    if stage >= _STOP_AT:
        raise _Stop()


class Sched:
    CE = ("pe", "act", "dve", "pool")

    def __init__(self, nc, sems, sp_sems, pool_sems, act_sems):
        self.nc = nc
        self.sems = sems
        self.streams = {e: [] for e in ("pe", "act", "dve", "pool", "sp")}
        self.cnt = {e: 0 for e in self.CE}
        self.seen = {e: {} for e in self.streams}
        self.res = {}
        self.dma_sems = {"sp": sp_sems, "pool": pool_sems, "act": act_sems}
        self.dma_tot = {}
        self.dma_rr = {"sp": 0, "pool": 0, "act": 0}
        self.semobj = dict(sems)
        for q, lst in self.dma_sems.items():
            for i, s in enumerate(lst):
                self.semobj[(q, i)] = s
                self.dma_tot[(q, i)] = 0

    def _deps(self, eng, reads, writes):
        need = {}

        def add(ev):
            if ev is None:
                return
            k, v = ev
            if need.get(k, 0) < v:
                need[k] = v
        for r in reads:
            st = self.res.get(r)
            if st is not None:
                add(st["w"])
        for w in writes:
            st = self.res.get(w)
            if st is not None:
                if st["w"] is not None and (st["w"][0] != eng or eng != "pe"):
                    add(st["w"])
                for k, v in st["r"].items():
                    if k != eng or eng != "pe":
                        add((k, v))
        waits = []
        for k, v in need.items():
            if self.seen[eng].get(k, 0) < v:
                self.seen[eng][k] = v
                waits.append((self.semobj[k], v))
        return waits

    def _mark(self, ev, reads, writes):
        for r in reads:
            st = self.res.setdefault(r, {"w": None, "r": {}})
            if st["r"].get(ev[0], 0) < ev[1]:
                st["r"][ev[0]] = ev[1]
        for w in writes:
            self.res[w] = {"w": ev, "r": {}}

    def op(self, eng, fn, reads=(), writes=(), inc=True, tag=None):
        waits = self._deps(eng, reads, writes)
        if tag is not None and os.environ.get("KDBG"):
            print("DBG", tag, eng, [(str(s_), v) for s_, v in waits], "cnt", dict(self.cnt))
        if inc:
            self.cnt[eng] += 1
            ev = (eng, self.cnt[eng])
        else:
            ev = (eng, self.cnt[eng] + 1)
        self._mark(ev, reads, writes)
        sem = self.sems[eng]

        def emit(E, waits=waits, fn=fn, inc=inc, sem=sem):
            for s, v in waits:
                E.wait_ge(s, v)
            ins = fn(E)
            if inc:
                ins.then_inc(sem, 1)
        self.streams[eng].append(emit)

    def dma(self, q, out, in_, reads=(), writes=()):
        lst = self.dma_sems[q]
        i = self.dma_rr[q] % len(lst)
        self.dma_rr[q] += 1
        key = (q, i)
        waits = self._deps(q, reads, writes)
        prev = self.dma_tot[key]
        if prev > 0 and self.seen[q].get(key, 0) < prev:
            self.seen[q][key] = prev
            waits.append((self.semobj[key], prev))
        self.dma_tot[key] = prev + 16
        ev = (key, prev + 16)
        self._mark(ev, reads, writes)
        sem = self.semobj[key]

        def emit(E, waits=waits, out=out, in_=in_, sem=sem):
            for s, v in waits:
                E.wait_ge(s, v)
            E.dma_start(out=out, in_=in_).then_inc(sem, 16)
        self.streams[q].append(emit)

    def barrier(self, keep=(), skip_queue=None):
        for e in self.streams:
            waits = []
            for k in self.CE:
                v = self.cnt[k]
                if v > 0 and k != e and self.seen[e].get(k, 0) < v:
                    self.seen[e][k] = v
                    waits.append((self.semobj[k], v))
            for k, v in self.dma_tot.items():
                if k[0] == skip_queue:
                    continue
                if v > 0 and self.seen[e].get(k, 0) < v:
                    self.seen[e][k] = v
                    waits.append((self.semobj[k], v))

            def emit(E, waits=waits):
                for s, v in waits:
                    E.wait_ge(s, v)
            self.streams[e].append(emit)
        self.res = {k: v for k, v in self.res.items() if isinstance(k, tuple) and k[0] in keep}

    def finish_waits(self):
        waits = [(self.semobj[k], v) for k, v in self.dma_tot.items() if v > 0]
        waits += [(self.semobj[k], self.cnt[k]) for k in self.CE if self.cnt[k] > 0]

        def emit(E, waits=waits):
            for s, v in waits:
                E.wait_ge(s, v)
        self.streams["sp"].append(emit)


class Arena:
    def __init__(self, ap, nelem):
        self.ap = ap
        self.n = nelem
        self.off = 0
        self.hi = 0

    def mark(self):
        return self.off

    def reset(self, m):
        self.off = m

    def take(self, nbytes):
        nbytes = (nbytes + 63) // 64 * 64
        n16 = nbytes // 2
        o = self.off
        self.off += n16
        self.hi = max(self.hi, self.off)
        assert self.off <= self.n, f"arena overflow {self.off * 2} > {self.n * 2}"
        return self.ap[:, o:o + n16]

    def bf(self, *shape):
        n = int(np.prod(shape))
        v = self.take(n * 2)[:, 0:n]
        return self._shape(v, shape)

    def f32(self, *shape):
        n = int(np.prod(shape))
        v = self.take(n * 4).bitcast(F32)[:, 0:n]
        return self._shape(v, shape)

    @staticmethod
    def _shape(v, shape):
        if len(shape) == 1:
            return v
        if len(shape) == 2:
            return v.rearrange("p (a b) -> p a b", a=shape[0])
        if len(shape) == 3:
            return v.rearrange("p (a b c) -> p a b c", a=shape[0], b=shape[1])
        raise ValueError(shape)


ARENA_BYTES = 212480


def build_program():
    nc = bass.Bass("TRN2", target_bir_lowering=False)
    x = nc.dram_tensor("x", [T, D], F32, kind="ExternalInput").ap()
    w_in = nc.dram_tensor("w_in", [D, INC], F32, kind="ExternalInput").ap()
    w_out = nc.dram_tensor("w_out", [D, D], F32, kind="ExternalInput").ap()
    w_gate = nc.dram_tensor("w_gate", [D, DFF], F32, kind="ExternalInput").ap()
    w_up = nc.dram_tensor("w_up", [D, DFF], F32, kind="ExternalInput").ap()
    w_down = nc.dram_tensor("w_down", [DFF, D], F32, kind="ExternalInput").ap()
    cst = nc.dram_tensor("cst", [128, NCST], F32, kind="ExternalInput").ap()
    out = nc.dram_tensor("out", [T, D], F32, kind="ExternalOutput").ap()
    h1s = nc.dram_tensor("h1s", [T, D], F32, kind="Internal").ap()

    w_in_v = w_in.rearrange("(k p) n -> p k n", p=128)
    w_out_v = w_out.rearrange("(k p) n -> p k n", p=128)
    w_gate_v = w_gate.rearrange("(k p) n -> p k n", p=128)
    w_up_v = w_up.rearrange("(k p) n -> p k n", p=128)
    w_down_v = w_down.rearrange("(k p) n -> p k n", p=128)

    from contextlib import ExitStack
    with ExitStack() as es:
        arena_t = es.enter_context(nc.sbuf_tensor("arena", [128, ARENA_BYTES // 2], BF16))
        banks = [es.enter_context(nc.psum_tensor(f"bank{i}", [128, 512], F32)) for i in range(8)]
        sems = {e: es.enter_context(nc.semaphore(f"s_{e}")) for e in Sched.CE}
        sp_sems = [es.enter_context(nc.semaphore(f"d_sp{i}")) for i in range(8)]
        pool_sems = [es.enter_context(nc.semaphore(f"d_pl{i}")) for i in range(6)]
        act_sems = [es.enter_context(nc.semaphore(f"d_ac{i}")) for i in range(4)]
        block = es.enter_context(nc.Block())
        S = Sched(nc, sems, sp_sems, pool_sems, act_sems)
        A = Arena(arena_t[:, :], ARENA_BYTES // 2)

        ident = A.bf(128)
        ones_bf = A.bf(128)
        csm = A.f32(144)
        ssq = A.f32(64)
        csh = A.f32(144)
        rs = A.f32(64)
        CW = csm[:, 0:124]
        CB = csm[:, 124:128]
        LG = csm[:, 128:132]
        LB = csm[:, 132:136]
        B31 = csm[:, 136:144]
        CWh = csh[:, 0:124]
        LGh = csh[:, 128:132]
        LBh = csh[:, 132:136]
        m_persist = A.mark()

        diag = A.bf(124, 128)
        UC = A.bf(4, 30 + G)
        Y = A.f32(4, G)
        Yb = [A.bf(G) for _ in range(2)]
        Ysq = [A.bf(G) for _ in range(2)]
        K2 = A.bf(8, SEQ)
        VA = A.bf(16, 8, 128)
        w_out_bf = A.bf(8, 1024)
        Q2 = A.bf(8, G)
        g1b = A.f32(1024)
        xt = [A.f32(1024) for _ in range(2)]
        u_bf = [A.bf(1024) for _ in range(4)]
        uT = A.bf(8, G)
        wblk = [A.bf(8, 256) for _ in range(4)]
        NE = 4
        Eb = [A.bf(G) for _ in range(NE)]
        mixT = uT
        tmpf = [A.f32(G) for _ in range(3)]
        MU = A.f32(G)
        RSTD = A.f32(G)
        TThi = A.bf(8, 256)
        TTlo = A.bf(8, 256)
        Rb = [A.f32(G) for _ in range(1)]
        G8x = A.f32(4, 8, 8)
        TOPx = Rb[0][:, 0:256].rearrange("p (c h b) -> p c h b", c=4, h=8)
        NMx = A.bf(4, 64)
        KS = A.bf(8, 8)
        KSf = A.f32(8, 8)
        print("phase A arena bytes:", A.hi * 2)
        TTf = Y
        TTf3 = TTf.rearrange("p a b -> p (a b)").rearrange("p (h j) -> p h j", h=8)

        MMb = banks
        MMRING = (0, 1, 4, 5)
        misc = banks[2]
        TPS = banks[3][:, :].bitcast(BF16)
        Sb = [banks[6], banks[7]]
        SRING = [3, 2, 4, 5]
        OPHYS = [0, 1, 6, 7]
        Ob = [banks[i] for i in OPHYS]
        G_ps4 = misc[:, 0:256]
        G_ps4v = G_ps4.rearrange("p (c h b) -> p c h b", c=4, h=8)
        NMT4 = misc[:, 256:512].bitcast(BF16)

        S.dma("sp", csm[:, 0:144], cst[:, C_CW:C_CW + 144], writes=["csm"])
        S.dma("sp", g1b, cst[:, C_G1:C_G1 + 1024], writes=["g1b"])
        S.op("pool", lambda E: E.memset(ident, 0.0), writes=["ident"])
        S.op("pool", lambda E: E.affine_select(out=ident, in_=ident, pattern=[[-1, 128]], compare_op=ALU.not_equal,
                                               fill=1.0, base=0, channel_multiplier=1), reads=["ident"], writes=["ident"])
        S.op("pool", lambda E: E.memset(ones_bf, 1.0), writes=["ones"])
        S.op("pool", lambda E: E.memset(KSf, 0.0), writes=[("KSf", bb) for bb in range(8)])
        def late_setup():
            S.dma("sp", TTf.rearrange("p a b -> p (a b)"), cst[:, C_TT:C_TT + 2048], writes=["Y0", "Y1", "Y2", "Y3"])
            for kc in range(0, 8, 4):
                S.dma("pool", w_out_bf[:, kc:kc + 4, :], w_out_v[:, kc:kc + 4, :], writes=[("wo", kc // 4)])
            S.op("dve", lambda E: E.tensor_scalar(out=csh[:, 0:136], in0=csm[:, 0:136], scalar1=0.5, scalar2=None, op0=ALU.mult),
                 reads=["csm"], writes=["csh"])
            for cc in range(4):
                S.op("dve", lambda E, cc=cc: E.tensor_tensor(
                    out=diag[:, cc * 31:(cc + 1) * 31, :],
                    in0=ident.unsqueeze(1).to_broadcast([128, 31, 128]),
                    in1=CWh[:, cc * 31:(cc + 1) * 31].unsqueeze(2).to_broadcast([128, 31, 128]),
                    op=ALU.mult), reads=["ident", "csh"], writes=[("diag", cc)])
            S.op("dve", lambda E: E.tensor_tensor(out=TTf3, in0=TTf3, in1=B31.unsqueeze(2).to_broadcast([128, 8, 256]),
                                                  op=ALU.subtract), reads=["Y0", "Y1", "Y2", "Y3", "csm"], writes=["Y0", "Y1", "Y2", "Y3"])
            S.op("dve", lambda E: E.tensor_copy(out=TThi, in_=TTf3), reads=["Y0", "Y1", "Y2", "Y3"], writes=["TThi"])
            S.op("dve", lambda E: E.tensor_tensor(out=TTf3, in0=TTf3, in1=TThi, op=ALU.subtract),
                 reads=["Y0", "Y1", "Y2", "Y3", "TThi"], writes=["Y0", "Y1", "Y2", "Y3"])
            S.op("dve", lambda E: E.tensor_copy(out=TTlo, in_=TTf3), reads=["Y0", "Y1", "Y2", "Y3"], writes=["TTlo"])
            S.op("dve", lambda E: E.memset(VA[:, :, :, 64:128], 1.0), writes=["VAones"])
            S.op("dve", lambda E: E.memset(K2[64:128, :, :], 1.0), writes=[("K2i", h) for h in range(8)])
            for h in range(8):
                kv = K2[64:128, h, :].rearrange("p (b t) -> p b t", b=8)
                S.op("pool", lambda E, kv=kv, h=h: E.affine_select(out=kv, in_=kv, pattern=[[-1, 8], [0, 256]],
                                                                   compare_op=ALU.is_equal, fill=0.0, base=-8 * h,
                                                                   channel_multiplier=1),
                     reads=[("K2i", h)], writes=[("K2i", h)])

        cnt = {"xt": 0, "mm": 0, "wblk": 0, "tmpf": 0, "yb": 0, "g8": 0, "tp": 0}

        def rmsnorm_to_bf(src_tile, src_key, dst_bf, dst_key, gb, gb_key, col):
            S.op("act", lambda E: E.activation(out=dst_bf, in_=src_tile, func=AF.Square, accum_out=ssq[:, col:col + 1]),
                 reads=[src_key], writes=[dst_key, ("ssq", col)])
            S.op("act", lambda E: E.activation(out=rs[:, col:col + 1], in_=ssq[:, col:col + 1], func=AF.Ln,
                                               scale=1.0 / D, bias=RMS_EPS),
                 reads=[("ssq", col)], writes=[("rs", col)])
            S.op("act", lambda E: E.activation(out=rs[:, col:col + 1], in_=rs[:, col:col + 1], func=AF.Exp, scale=-0.5),
                 reads=[("rs", col)], writes=[("rs", col)])
            S.op("dve", lambda E: E.scalar_tensor_tensor(out=dst_bf, in0=src_tile, scalar=rs[:, col:col + 1], in1=gb,
                                                         op0=ALU.mult, op1=ALU.mult),
                 reads=[src_key, ("rs", col), gb_key], writes=[dst_key])

        TPSv = {i: banks[i][:, :].bitcast(BF16) for i in (3, 2, 6, 7)}

        def transposes(src_bf, src_key, dstT, dst_key, tc):
            bk = (3, 2, 6, 7)[cnt["tp"] % 4]
            cnt["tp"] += 1
            tps = TPSv[bk]
            for kc in range(8):
                S.op("pe", lambda E, kc=kc: E.transpose(tps[:, kc * 128:(kc + 1) * 128], src_bf[:, kc * 128:(kc + 1) * 128], ident),
                     reads=[src_key, "ident"], writes=[("pb", bk)], inc=(kc == 7))
            S.op("dve", lambda E: E.tensor_copy(out=dstT[:, :, tc * 128:(tc + 1) * 128],
                                                in_=tps.rearrange("p (k t) -> p k t", k=8)),
                 reads=[("pb", bk)], writes=[(dst_key, tc)] + [("mixT", kc, hf, tc) for kc in range(8) for hf in range(2)])

        def a1a_chunk(g, tc):
            c = 4 * g + tc
            b = cnt["xt"] % 2
            cnt["xt"] += 1
            S.dma("sp", xt[b], x[c * 128:(c + 1) * 128, :], writes=[("xt", b)])
            rmsnorm_to_bf(xt[b], ("xt", b), u_bf[tc], ("u_bf", tc), g1b, "g1b", tc)

        def a1a(g):
            for tc in range(4):
                a1a_chunk(g, tc)

        def a1b(g):
            for tc in range(4):
                transposes(u_bf[tc], ("u_bf", tc), uT, "uT", tc)

        BLK_ORDER = [("g", 0), ("a", 0), ("g", 1), ("a", 1), ("k", 0), ("k", 1), ("q", 0), ("q", 1), ("v", 0), ("v", 1)]
        BLK_COL = {"q": 0, "k": 512, "v": 1024, "a": 1536, "g": 2048}

        def issue_loads(upto):
            while cnt["wblk"] < min(upto, 10 * NG):
                idx = cnt["wblk"]
                cnt["wblk"] += 1
                name, half = BLK_ORDER[idx % 10]
                col0 = BLK_COL[name] + 256 * half
                wb = idx % 4
                S.dma("pool", wblk[wb], w_in_v[:, :, col0:col0 + 256], writes=[("wblk", wb)])

        def phase_a_part1(g):
            gi = g % 4
            t0 = gi * G
            uT_keys = [("uT", tc) for tc in range(4)]

            def use_blk(name, half, pair_with_prev=False):
                idx = 10 * g + BLK_ORDER.index((name, half))
                issue_loads((idx - 1 if pair_with_prev else idx) + 4)
                return idx % 4

            def fm_group(wb, j, per_chunk=False):
                mb = MMRING[cnt["mm"] % 4]
                cnt["mm"] += 1
                if per_chunk:
                    for tc in range(4):
                        for kc in range(8):
                            S.op("pe", lambda E, kc=kc, mb=mb, tc=tc: E.matmul(
                                MMb[mb][:, tc * 128:(tc + 1) * 128], lhsT=wblk[wb][:, kc, j * 128:(j + 1) * 128],
                                rhs=uT[:, kc, tc * 128:(tc + 1) * 128], start=(kc == 0), stop=(kc == 7), skip_group_check=True),
                                reads=[("wblk", wb), ("uT", tc)], writes=[("pb", mb)], inc=(kc == 7 and tc == 3))
                    return mb
                for kc in range(8):
                    S.op("pe", lambda E, kc=kc, mb=mb: E.matmul(MMb[mb][:, :], lhsT=wblk[wb][:, kc, j * 128:(j + 1) * 128],
                                                               rhs=uT[:, kc, :], start=(kc == 0), stop=(kc == 7),
                                                               skip_group_check=True),
                         reads=[("wblk", wb)] + uT_keys, writes=[("pb", mb)], inc=(kc == 7))
                return mb

            wb_g = [use_blk("g", 0), None]
            wb_a = [use_blk("a", 0, True), None]
            if gi == 0:
                S.op("pool", lambda E: E.memset(UC[:, :, 0:30], 0.0), writes=[("UCh", cc) for cc in range(4)])
            else:
                S.op("pool", lambda E: E.tensor_copy(out=UC[:, :, 0:30], in_=UC[:, :, G:G + 30]),
                     reads=[("UC", cc) for cc in range(4)], writes=[("UCh", cc) for cc in range(4)])
            sig_of = {}
            for cc in range(4):
                if cc == 2:
                    wb_g[1] = use_blk("g", 1)
                    wb_a[1] = use_blk("a", 1, True)
                mb = fm_group(wb_g[cc // 2], cc % 2, per_chunk=(cc == 0))
                r = cnt["tmpf"] % 3
                cnt["tmpf"] += 1
                S.op("act", lambda E, mb=mb, r=r: E.activation(out=tmpf[r], in_=MMb[mb][:, :], func=AF.Tanh, scale=0.5),
                     reads=[("pb", mb)], writes=[("tmpf", r)])
                mb2 = fm_group(wb_a[cc // 2], cc % 2)
                S.op("dve", lambda E, mb2=mb2, r=r, cc=cc: E.scalar_tensor_tensor(out=UC[:, cc, 30:30 + G], in0=tmpf[r], scalar=1.0,
                                                                                in1=MMb[mb2][:, :], op0=ALU.add, op1=ALU.mult),
                     reads=[("pb", mb2), ("tmpf", r)], writes=[("UC", cc)])
            if g + 1 < NG:
                a1a_chunk(g + 1, 0)
                a1a_chunk(g + 1, 1)
            for j in range(4):
                if j % 2 == 0:
                    wb_k = use_blk("k", j // 2)
                mb = fm_group(wb_k, j % 2)
                S.op("dve", lambda E, mb=mb, j=j: E.tensor_copy(out=K2[0:64, 2 * j, t0:t0 + G], in_=MMb[mb][0:64, :]),
                     reads=[("pb", mb)], writes=[("K2", 2 * j)])
                S.op("dve", lambda E, mb=mb, j=j: E.tensor_copy(out=K2[0:64, 2 * j + 1, t0:t0 + G], in_=MMb[mb][64:128, :]),
                     reads=[("pb", mb)], writes=[("K2", 2 * j + 1)])
            for bb in (2 * gi, 2 * gi + 1):
                S.op("dve", lambda E, bb=bb: E.tensor_reduce(out=KSf[0:64, :, bb:bb + 1],
                                                            in_=K2[0:64, :, bb * 256:(bb + 1) * 256], axis=AX.X, op=ALU.add),
                     reads=[("K2", h) for h in range(8)], writes=[("KSf", bb)])
            S.op("dve", lambda E: E.tensor_copy(out=KS[0:64, :, :], in_=KSf[0:64, :, :]),
                 reads=[("KSf", bb) for bb in range(8)], writes=[("KS", bb) for bb in range(8)])

            if g + 1 < NG:
                a1a_chunk(g + 1, 2)
            for j in range(4):
                if j % 2 == 0:
                    wb_q = use_blk("q", j // 2)
                mb = fm_group(wb_q, j % 2)
                S.op("act", lambda E, mb=mb, j=j: E.activation(out=Q2[0:64, 2 * j, :], in_=MMb[mb][0:64, :], func=AF.Copy, scale=0.125),
                     reads=[("pb", mb)], writes=[("Q2", 2 * j)])
                S.op("act", lambda E, mb=mb, j=j: E.activation(out=Q2[0:64, 2 * j + 1, :], in_=MMb[mb][64:128, :], func=AF.Copy, scale=0.125),
                     reads=[("pb", mb)], writes=[("Q2", 2 * j + 1)])
            if g + 1 < NG:
                a1a_chunk(g + 1, 3)
            wb_v = [use_blk("v", 0), use_blk("v", 1, True)]
            for tc in range(4):
                mb = MMRING[cnt["mm"] % 4]
                cnt["mm"] += 1
                cs = 4 * gi + tc
                for hv in range(2):
                    for kc in range(8):
                        S.op("pe", lambda E, kc=kc, mb=mb, tc=tc, hv=hv: E.matmul(
                            MMb[mb][:, hv * 256:(hv + 1) * 256], lhsT=uT[:, kc, tc * 128:(tc + 1) * 128],
                            rhs=wblk[wb_v[hv]][:, kc, :], start=(kc == 0), stop=(kc == 7), skip_group_check=True),
                            reads=[("wblk", wb_v[hv]), ("uT", tc)], writes=[("pb", mb)], inc=(kc == 7 and hv == 1))
                S.op("dve", lambda E, mb=mb, cs=cs: E.tensor_copy(out=VA[:, cs, :, 0:64],
                                                                 in_=MMb[mb][:, :].rearrange("p (h d) -> p h d", h=8)),
                     reads=[("pb", mb)], writes=[("VA", cs)])

            if g == 0:
                late_setup()
            _chk(2 + 10 * (g > 0))
            Q2m_keys = [("Q2m", hh) for hh in range(8)]
            R_keys = [("R", tq) for tq in range(4)]
            KS_keys = [("KS", bb) for bb in range(8)]
            if gi < 2:
                S.op("pool", lambda E: E.memset(Q2[64:128, :, :], 0.0), writes=Q2m_keys)
            else:
                owns = [(4 * gi + tc) // 2 for tc in range(4)]
                S.op("pool", lambda E: E.memset(G8x, -1e30), writes=["G8x"])
                for tc in range(4):
                    for h in range(8):
                        S.op("pe", lambda E, h=h, tc=tc, own=owns[tc]: E.matmul(
                            G_ps4[:, tc * 64 + h * 8:tc * 64 + h * 8 + own], lhsT=Q2[0:64, h, tc * 128:(tc + 1) * 128],
                            rhs=KS[0:64, h, 0:own], start=True, stop=True, skip_group_check=True),
                            reads=[("Q2", h)] + KS_keys, writes=[("pb", 2)], inc=(tc == 3 and h == 7))
                for pr in range(2):
                    S.op("dve", lambda E, pr=pr, own=owns[2 * pr]: E.tensor_copy(
                        out=G8x[:, 2 * pr:2 * pr + 2, :, 0:own], in_=G_ps4v[:, 2 * pr:2 * pr + 2, :, 0:own]),
                        reads=[("pb", 2), "G8x"], writes=["G8x"])
                for tc in range(4):
                    for h in range(8):
                        S.op("dve", lambda E, tc=tc, h=h: E.max(out=TOPx[:, tc, h, :], in_=G8x[:, tc, h, :]),
                             reads=["G8x"], writes=R_keys)
                S.op("dve", lambda E: E.tensor_tensor(out=G8x, in0=G8x, in1=TOPx[:, :, :, 2:3].to_broadcast([128, 4, 8, 8]),
                                                      op=ALU.is_lt), reads=["G8x"] + R_keys, writes=["G8x"])
                for pr in range(2):
                    S.op("dve", lambda E, pr=pr, own=owns[2 * pr]: E.memset(G8x[:, 2 * pr:2 * pr + 2, :, own:own + 1], 0.0),
                         reads=["G8x"], writes=["G8x"])
                S.op("dve", lambda E: E.tensor_scalar(out=NMx.rearrange("p c j -> p (c j)"), in0=G8x.rearrange("p c h b -> p (c h b)"),
                                                      scalar1=NEG, scalar2=None, op0=ALU.mult),
                     reads=["G8x"], writes=["NMx"])

            _chk(2.1 + 10 * (g > 0))
            def gate_b():
                if gi < 2:
                    return
                for tc in range(4):
                    S.op("pe", lambda E, tc=tc: E.transpose(NMT4[0:64, tc * 128:(tc + 1) * 128], NMx[:, tc, :], ident),
                         reads=["NMx", "ident"], writes=[("pb", 2)], inc=(tc == 3))
                for hh in range(8):
                    S.op("act", lambda E, hh=hh: E.activation(out=Q2[64:128, hh, :], in_=NMT4[0:64, :], func=AF.Copy),
                         reads=[("pb", 2)], writes=[("Q2m", hh)])


            pend_stats = []

            def emit_stats(cc, yb):
                S.op("pe", lambda E: E.matmul(Sb[0][:, :], lhsT=ones_bf, rhs=Yb[yb], start=(cc == 0), stop=(cc == 3),
                                              skip_group_check=True),
                     reads=["ones", ("Yb", yb)], writes=[("pb", 6)], inc=True)
                S.op("pe", lambda E: E.matmul(Sb[1][:, :], lhsT=ones_bf, rhs=Ysq[yb], start=(cc == 0), stop=(cc == 3),
                                              skip_group_check=True),
                     reads=["ones", ("Ysq", yb)], writes=[("pb", 7)], inc=True)
            for cc in range(4):
                mb = cc % 2
                for t in range(31):
                    S.op("pe", lambda E, t=t, mb=mb, cc=cc: E.matmul(MMb[mb][:, :], lhsT=diag[:, cc * 31 + t, :],
                                                                    rhs=UC[:, cc, t:t + G], start=(t == 0), stop=(t == 30),
                                                                    skip_group_check=True),
                         reads=[("diag", cc), ("UC", cc), ("UCh", cc)], writes=[("pb", mb)], inc=(t == 30),
                         tag=(("conv0", g) if (t == 0 and cc == 0) else None))
                yb = cnt["yb"] % 2
                cnt["yb"] += 1
                S.op("act", lambda E, mb=mb, cc=cc, yb=yb: E.activation(out=Yb[yb], in_=MMb[mb][:, :], func=AF.Identity,
                                                                       bias=CB[:, cc:cc + 1]),
                     reads=[("pb", mb), "csm"], writes=[("Yb", yb)])
                S.op("act", lambda E, mb=mb, cc=cc, yb=yb: E.activation(out=Ysq[yb], in_=MMb[mb][:, :], func=AF.Square,
                                                                       bias=CB[:, cc:cc + 1]),
                     reads=[("pb", mb), "csm"], writes=[("Ysq", yb)])
                S.op("act", lambda E, mb=mb, cc=cc: E.activation(out=Y[:, cc, :], in_=MMb[mb][:, :], func=AF.Identity,
                                                                bias=CB[:, cc:cc + 1]),
                     reads=[("pb", mb), "csm"], writes=[f"Y{cc}"])
                pend_stats.append((cc, yb))
                if len(pend_stats) > 1:
                    emit_stats(*pend_stats.pop(0))
                if cc == 2:
                    gate_b()
            emit_stats(*pend_stats.pop(0))
            _chk(2.2 + 10 * (g > 0))
            S.op("act", lambda E: E.activation(out=MU, in_=Sb[0][:, :], func=AF.Copy, scale=1.0 / 512), reads=[("pb", 6)], writes=["MU"])
            rq = cnt["tmpf"] % 3
            cnt["tmpf"] += 1
            S.op("dve", lambda E: E.tensor_tensor(out=tmpf[rq], in0=MU, in1=MU, op=ALU.mult), reads=["MU"], writes=[("tmpf", rq)])
            S.op("dve", lambda E: E.scalar_tensor_tensor(out=RSTD, in0=Sb[1][:, :], scalar=1.0 / 512, in1=tmpf[rq],
                                                         op0=ALU.mult, op1=ALU.subtract),
                 reads=[("pb", 7), ("tmpf", rq)], writes=["RSTD"])

        def ln_rstd(g):
            S.op("act", lambda E: E.activation(out=RSTD, in_=RSTD, func=AF.Ln, bias=LN_EPS), reads=["RSTD"], writes=["RSTD"])
            S.op("act", lambda E: E.activation(out=RSTD, in_=RSTD, func=AF.Exp, scale=-0.5), reads=["RSTD"], writes=["RSTD"])

        ln_slot = {}

        def ln_pre(g, cc):
            r = cnt["tmpf"] % 3
            cnt["tmpf"] += 1
            ln_slot[cc] = r
            S.op("dve", lambda E, cc=cc, r=r: E.tensor_tensor(out=tmpf[r], in0=Y[:, cc, :], in1=MU, op=ALU.subtract),
                 reads=[f"Y{cc}", "MU"], writes=[("tmpf", r)])
            S.op("pool", lambda E, r=r: E.tensor_tensor(out=tmpf[r], in0=tmpf[r], in1=RSTD, op=ALU.mult),
                 reads=[("tmpf", r), "RSTD"], writes=[("tmpf", r)])
            S.op("pool", lambda E, cc=cc, r=r: E.tensor_scalar(out=tmpf[r], in0=tmpf[r], scalar1=LGh[:, cc:cc + 1], scalar2=LBh[:, cc:cc + 1],
                                                             op0=ALU.mult, op1=ALU.add),
                 reads=[("tmpf", r), "csh"], writes=[("tmpf", r)])

        def ln_apply(g, ccs):
            uT_keys = [("uT", tc) for tc in range(4)]
            for cc in ccs:
                r = ln_slot[cc]
                S.op("act", lambda E, cc=cc, r=r: E.activation(out=Y[:, cc, :], in_=tmpf[r], func=AF.Tanh),
                     reads=[("tmpf", r)], writes=[f"Y{cc}"])
            for cc in ccs:
                r = ln_slot[cc]
                S.op("dve", lambda E, cc=cc, r=r: E.scalar_tensor_tensor(out=mixT[:, 4 + cc, :], in0=Y[:, cc, :], scalar=1.0, in1=tmpf[r],
                                                                       op0=ALU.add, op1=ALU.mult),
                     reads=[f"Y{cc}", ("tmpf", r)],
                     writes=[("mixT", 4 + cc, hf, tq) for hf in range(2) for tq in range(4)] + uT_keys)

        def phase_a_part2(g):
            gi = g % 4
            uT_keys = [("uT", tc) for tc in range(4)]
            _chk(4 + 10 * (g > 0))
            steps = []
            for h in range(8):
                first_kc = 4 * gi
                order = [first_kc] + list(range(0, first_kc)) + [first_kc + 1, first_kc + 2, first_kc + 3]
                for i, kc in enumerate(order):
                    j = kc - first_kc
                    q0 = max(j, 0) * 128
                    tadd = None
                    if j >= 0:
                        ncols = 256 if j < 3 else 128
                        tadd = (0, ncols, q0)
                    elif j == -1:
                        tadd = (128, 128, 0)
                    steps.append(dict(h=h, kc=kc, q0=q0, tadd=tadd, first=(i == 0), last=(i == len(order) - 1), idx=len(steps)))

            def emit_qk(st):
                h, kc, q0, sb = st["h"], st["kc"], st["q0"], st["idx"] % len(SRING)
                tadd = st["tadd"]
                S.op("pe", lambda E: E.matmul(banks[SRING[sb]][:, q0:G], lhsT=K2[:, h, kc * 128:(kc + 1) * 128], rhs=Q2[:, h, q0:G],
                                              start=True, stop=(tadd is None), skip_group_check=True),
                     reads=[("K2", h), ("K2i", h), ("Q2", h), ("Q2m", h)], writes=[("pb", SRING[sb])],
                     inc=(tadd is None))
                if tadd is not None:
                    c0, n, s0 = tadd
                    S.op("pe", lambda E: E.matmul(banks[SRING[sb]][:, s0:s0 + n], lhsT=ident, rhs=TThi[:, h, c0:c0 + n],
                                                  start=False, stop=False, skip_group_check=True),
                         reads=["ident", "TThi"], writes=[("pb", SRING[sb])], inc=False)
                    S.op("pe", lambda E: E.matmul(banks[SRING[sb]][:, s0:s0 + n], lhsT=ident, rhs=TTlo[:, h, c0:c0 + n],
                                                  start=False, stop=True, skip_group_check=True),
                         reads=["ident", "TTlo"], writes=[("pb", SRING[sb])], inc=True)

            def emit_exp(st):
                q0, sb, e = st["q0"], st["idx"] % len(SRING), st["idx"] % NE
                S.op("act", lambda E: E.activation(out=Eb[e][:, q0:G], in_=banks[SRING[sb]][:, q0:G], func=AF.Exp),
                     reads=[("pb", SRING[sb])], writes=[("E", e)])

            def emit_pv(st):
                h, kc, q0, e, ob = st["h"], st["kc"], st["q0"], st["idx"] % NE, st["h"] % 4
                cs = kc
                S.op("pe", lambda E: E.matmul(Ob[ob][:, q0:G], lhsT=VA[:, cs, h, :], rhs=Eb[e][:, q0:G],
                                              start=st["first"], stop=st["last"], skip_group_check=True),
                     reads=[("VA", cs), "VAones", ("E", e)], writes=[("pb", OPHYS[ob])], inc=True)
                if st["last"]:
                    p0 = (h % 2) * 64
                    pieces = [(tq * 128, (tq + 1) * 128, [tq]) for tq in range(4)] if h == 7 else [(0, G, [0, 1, 2, 3])]
                    for (a, bnd, tqs) in pieces:
                        S.op("dve", lambda E, a=a, bnd=bnd: E.reciprocal(out=Rb[0][0:64, a:bnd], in_=Ob[ob][64:128, a:bnd]),
                             reads=[("pb", OPHYS[ob])], writes=[("R", tq) for tq in tqs])
                        S.op("dve", lambda E, a=a, bnd=bnd: E.tensor_tensor(out=mixT[p0:p0 + 64, h // 2, a:bnd], in0=Ob[ob][0:64, a:bnd],
                                                                          in1=Rb[0][0:64, a:bnd], op=ALU.mult),
                             reads=[("pb", OPHYS[ob])] + [("R", tq) for tq in tqs],
                             writes=[("mixT", h // 2, h % 2, tq) for tq in tqs] + [("uT", tq) for tq in tqs])

            LA = 3
            pend = []
            preload = {}
            xpre = {}
            for st in steps:
                emit_qk(st)
                emit_exp(st)
                pend.append(st)
                if len(pend) > LA:
                    emit_pv(pend.pop(0))
                if st["last"]:
                    h = st["h"]
                    if h == 0:
                        ln_rstd(g)
                    hl = 2 if gi == 0 else 4
                    if h == hl:
                        ln_pre(g, 0)
                        ln_pre(g, 1)
                        ln_pre(g, 2)
                    if h == hl + 1:
                        ln_apply(g, (0, 1, 2))
                        ln_pre(g, 3)
                    if h == hl + 2:
                        ln_apply(g, (3,))
                    if h == 6:
                        for tq in range(2):
                            b = cnt["xt"] % 2
                            cnt["xt"] += 1
                            S.dma("sp", xt[b], x[(4 * g + tq) * 128:(4 * g + tq + 1) * 128, :], writes=[("xt", b)])
                            preload[tq] = b
            while pend:
                emit_pv(pend.pop(0))

            _chk(5 + 10 * (g > 0))
            for tc in range(4):
                c = 4 * g + tc
                mix_keys = [("mixT", kc, hf, tc) for kc in range(8) for hf in range(2)]
                if tc in preload:
                    b = preload[tc]
                else:
                    b = cnt["xt"] % 2
                    cnt["xt"] += 1
                    S.dma("sp", xt[b], x[c * 128:(c + 1) * 128, :], writes=[("xt", b)])
                for half in range(2):
                    mb = (0, 1, 4, 5)[(2 * tc + half) % 4]
                    for kc in range(8):
                        S.op("pe", lambda E, kc=kc, mb=mb, tc=tc, half=half: E.matmul(
                            banks[mb][:, :], lhsT=mixT[:, kc, tc * 128:(tc + 1) * 128], rhs=w_out_bf[:, kc, half * 512:(half + 1) * 512],
                            start=(kc == 0), stop=(kc == 7), skip_group_check=True),
                            reads=[("mixT", kc, 0, tc), ("mixT", kc, 1, tc), ("wo", kc // 4)],
                            writes=[("pb", mb)], inc=(kc == 7))
                    S.op("dve", lambda E, mb=mb, b=b, half=half: E.tensor_tensor(
                        out=xt[b][:, half * 512:(half + 1) * 512], in0=banks[mb][:, :], in1=xt[b][:, half * 512:(half + 1) * 512], op=ALU.add),
                        reads=[("pb", mb), ("xt", b)], writes=[("xt", b)])
                S.dma("act", h1s[c * 128:(c + 1) * 128, :], xt[b], reads=[("xt", b)], writes=[("h1s", c)])

        try:
          _chk(0)
          a1a(0)
          a1b(0)
          for g in range(NG):
            phase_a_part1(g)
            phase_a_part2(g)
            if g + 1 < NG:
                a1b(g + 1)
          _chk(20)
        except _Stop:
          pass

        m_end_a = A.mark()
        A.reset(m_persist)
        wu_bf = A.bf(8, DFF)
        wg_bf = A.bf(8, DFF)
        colblocks = [(i * 512, min(512, DFF - i * 512)) for i in range((DFF + 511) // 512)]
        dead_u = [("diag", cc) for cc in range(4)] + [("UC", cc) for cc in range(4)] + [("UCh", cc) for cc in range(4)] \
            + [f"Y{cc}" for cc in range(4)] + [("Yb", i) for i in range(2)] + [("Ysq", i) for i in range(2)]
        dead_g = [("Yb", i) for i in range(2)] + [("Ysq", i) for i in range(2)] + [("K2", h) for h in range(8)] \
            + [("K2i", h) for h in range(8)] + [("VA", cs) for cs in range(16)] + ["VAones"]

        def load_w(kind, bi, extra=()):
            c0, n = colblocks[bi]
            dst, src = (wg_bf, w_gate_v) if kind == "wg" else (wu_bf, w_up_v)
            for kc in range(0, 8, 4):
                S.dma("pool", dst[:, kc:kc + 4, c0:c0 + n], src[:, kc:kc + 4, c0:c0 + n],
                      writes=[(kind, bi, kc // 4)] + list(extra))
        PRE_U, PRE_G = 3, 2
        if _STOP_AT > 900:
            for bi in range(PRE_U):
                load_w("wu", bi, dead_u)
            for bi in range(PRE_G):
                load_w("wg", bi, dead_g)
        S.barrier(keep=("wg", "wu"), skip_queue="pool")
        wd_bf = A.bf(NFC, 1024)
        g2b = A.f32(1024)
        g3b = A.f32(1024)
        ht = [A.f32(1024) for _ in range(2)]
        hb = [A.f32(1024) for _ in range(2)]
        u2_bf = [A.bf(1024) for _ in range(4)]
        u2Tb = [A.bf(8, G) for _ in range(2)]
        hT = A.bf(NFC, G)
        SG = [A.bf(G) for _ in range(2)]
        print("phase B arena bytes:", A.hi * 2, "now", A.off * 2)

        pre_done = (_STOP_AT > 900)
        for bi in range(len(colblocks)):
            if not (pre_done and bi < PRE_G):
                load_w("wg", bi)
            if not (pre_done and bi < PRE_U):
                load_w("wu", bi)
        dchunks = [(0, 6), (6, 6), (12, 5), (17, 5)]
        for di, (f0, nf) in enumerate(dchunks):
            S.dma("pool", wd_bf[:, f0:f0 + nf, :], w_down_v[:, f0:f0 + nf, :], writes=[("wd", di)])

        def wd_key(fc):
            for di, (f0, nf) in enumerate(dchunks):
                if f0 <= fc < f0 + nf:
                    return ("wd", di)

        PG = [banks[0], banks[1]]
        PU = [banks[2], banks[3]]
        PD = [banks[4], banks[5]]
        TPSB = banks[6][:, :].bitcast(BF16)
        cntb = {"ht": 0, "hb": 0, "p": 0, "pd": 0, "sg": 0, "u2": 0, "tp": 0}

        TPSBv = {i: banks[i][:, :].bitcast(BF16) for i in (6, 7, 0, 2)}

        def transposes_b(src_bf, src_key, dstT, dst_key, tc):
            bk = (6, 7, 0, 2)[cntb["tp"] % 4]
            cntb["tp"] += 1
            tps = TPSBv[bk]
            for kc in range(8):
                S.op("pe", lambda E, kc=kc: E.transpose(tps[:, kc * 128:(kc + 1) * 128], src_bf[:, kc * 128:(kc + 1) * 128], ident),
                     reads=[src_key, "identB"], writes=[("pbB", bk)], inc=(kc == 7))
            S.op("dve", lambda E: E.tensor_copy(out=dstT[:, :, tc * 128:(tc + 1) * 128],
                                                in_=tps.rearrange("p (k t) -> p k t", k=8)),
                 reads=[("pbB", bk)], writes=[(dst_key, tc)])

        def b1a_chunk(g, tc):
            c = 4 * g + tc
            b = cntb["hb"] % 2
            cntb["hb"] += 1
            S.dma("sp", hb[b], h1s[c * 128:(c + 1) * 128, :], reads=[("h1s", c)], writes=[("hb", b)])
            rmsnorm_to_bf(hb[b], ("hb", b), u2_bf[tc], ("u2_bf", tc), g2b, "g2b", 8 + b)

        def b1a(g):
            for tc in range(4):
                b1a_chunk(g, tc)

        def b1a0():
            slots = ((hb[0], ("hb", 0), 8), (hb[1], ("hb", 1), 9), (ht[0], ("ht", 0), 10), (ht[1], ("ht", 1), 11))
            for tc in range(4):
                buf, key, col = slots[tc]
                S.dma("sp", buf, h1s[tc * 128:(tc + 1) * 128, :], reads=[("h1s", tc)], writes=[key])
            S.dma("sp", g2b, cst[:, C_G2:C_G2 + 1024], writes=["g2b"])
            S.dma("sp", g3b, cst[:, C_G3:C_G3 + 1024], writes=["g3b"])
            for tc in range(4):
                buf, key, col = slots[tc]
                rmsnorm_to_bf(buf, key, u2_bf[tc], ("u2_bf", tc), g2b, "g2b", col)

        def b1b(g):
            for tc in range(4):
                transposes_b(u2_bf[tc], ("u2_bf", tc), u2Tb[g % 2], ("u2T", g % 2), tc)

        def b2(g):
            u2T = u2Tb[g % 2]
            u2T_keys = [(("u2T", g % 2), tc) for tc in range(4)]
            for fc in range(NFC):
                bi = (fc * 128) // 512
                pb = cntb["p"] % 2
                cntb["p"] += 1
                for kc in range(8):
                    S.op("pe", lambda E, kc=kc, pb=pb, fc=fc: E.matmul(PG[pb][:, :], lhsT=wg_bf[:, kc, fc * 128:(fc + 1) * 128],
                                                                      rhs=u2T[:, kc, :], start=(kc == 0), stop=(kc == 7),
                                                                      skip_group_check=True),
                         reads=[("wg", bi, kc // 4)] + u2T_keys, writes=[("pbB", pb)], inc=(kc == 7))
                for kc in range(8):
                    S.op("pe", lambda E, kc=kc, pb=pb, fc=fc: E.matmul(PU[pb][:, :], lhsT=wu_bf[:, kc, fc * 128:(fc + 1) * 128],
                                                                      rhs=u2T[:, kc, :], start=(kc == 0), stop=(kc == 7),
                                                                      skip_group_check=True),
                         reads=[("wu", bi, kc // 4)] + u2T_keys, writes=[("pbB", 2 + pb)], inc=(kc == 7))
                sg = cntb["sg"] % 2
                cntb["sg"] += 1
                S.op("act", lambda E, pb=pb, sg=sg: E.activation(out=SG[sg], in_=PG[pb][:, :], func=AF.Silu),
                     reads=[("pbB", pb)], writes=[("SG", sg)])
                S.op("dve", lambda E, pb=pb, sg=sg, fc=fc: E.tensor_tensor(out=hT[:, fc, :], in0=PU[pb][:, :], in1=SG[sg], op=ALU.mult),
                     reads=[("pbB", 2 + pb), ("SG", sg)], writes=[("hT", fc)])
                if g + 1 < NG and fc % 4 == 1 and fc // 4 < 4:
                    b1a_chunk(g + 1, fc // 4)

        def b3(g):
            for tc in range(4):
                c = 4 * g + tc
                b = cntb["ht"] % 2
                cntb["ht"] += 1
                S.dma("sp", ht[b], h1s[c * 128:(c + 1) * 128, :], reads=[("h1s", c)], writes=[("ht", b)])
                for half in range(2):
                    pd = cntb["pd"] % 2
                    cntb["pd"] += 1
                    for fc in range(NFC):
                        S.op("pe", lambda E, fc=fc, pd=pd, tc=tc, half=half: E.matmul(
                            PD[pd][:, :], lhsT=hT[:, fc, tc * 128:(tc + 1) * 128], rhs=wd_bf[:, fc, half * 512:(half + 1) * 512],
                            start=(fc == 0), stop=(fc == NFC - 1), skip_group_check=True),
                            reads=[("hT", fc), wd_key(fc)], writes=[("pbB", 4 + pd)], inc=(fc == NFC - 1))
                    S.op("dve", lambda E, pd=pd, b=b, half=half: E.tensor_tensor(
                        out=ht[b][:, half * 512:(half + 1) * 512], in0=PD[pd][:, :], in1=ht[b][:, half * 512:(half + 1) * 512], op=ALU.add),
                        reads=[("pbB", 4 + pd), ("ht", b)], writes=[("ht", b)])
                col = 16 + b
                jk = (tc + 2) % 4
                S.op("act", lambda E, b=b, col=col, jk=jk: E.activation(out=u2_bf[jk], in_=ht[b], func=AF.Square, accum_out=ssq[:, col:col + 1]),
                     reads=[("ht", b)], writes=[("u2_bf", jk), ("ssq", col)])
                S.op("act", lambda E, col=col: E.activation(out=rs[:, col:col + 1], in_=ssq[:, col:col + 1], func=AF.Ln,
                                                           scale=1.0 / D, bias=RMS_EPS),
                     reads=[("ssq", col)], writes=[("rs", col)])
                S.op("act", lambda E, col=col: E.activation(out=rs[:, col:col + 1], in_=rs[:, col:col + 1], func=AF.Exp, scale=-0.5),
                     reads=[("rs", col)], writes=[("rs", col)])
                S.op("dve", lambda E, b=b, col=col: E.scalar_tensor_tensor(out=ht[b], in0=ht[b], scalar=rs[:, col:col + 1], in1=g3b,
                                                                          op0=ALU.mult, op1=ALU.mult),
                     reads=[("ht", b), ("rs", col), "g3b"], writes=[("ht", b)])
                S.dma("act", out[c * 128:(c + 1) * 128, :], ht[b], reads=[("ht", b)], writes=[("out", c)])

        try:
          _chk(21)
          b1a0()
          b1b(0)
          for g in range(NG):
            b2(g)
            if g + 1 < NG:
                b1b(g + 1)
            b3(g)
        except _Stop:
          pass
        S.finish_waits()

        def run(stream):
            def f(E):
                for emit in stream:
                    emit(E)
            return f
        block.sync(run(S.streams["sp"]))
        block.gpsimd(run(S.streams["pool"]))
        block.scalar(run(S.streams["act"]))
        block.vector(run(S.streams["dve"]))
        block.tensor(run(S.streams["pe"]))
        print("instr counts:", {k: len(v) for k, v in S.streams.items()}, "sem counts:", S.cnt)
    return nc


def _t5_bucket(d):
    max_exact = 16
    dd = np.maximum(d, 1).astype(np.float32)
    large = max_exact + (np.log(dd / np.float32(max_exact)) / np.float32(math.log(128 / max_exact))
                         * np.float32(32 - max_exact)).astype(np.int32)
    large = np.minimum(large, 31)
    return np.where(d < max_exact, d, large)


def _build_consts(mix_norm_g, rel_bias, conv_w, conv_b, conv_ln_g, conv_ln_b, ffn_norm_g, final_norm_g):
    c = np.zeros((128, NCST), np.float32)
    c[:, C_G1:C_G1 + 1024] = np.broadcast_to(mix_norm_g.reshape(1, 1024), (128, 1024))
    c[:, C_G2:C_G2 + 1024] = np.broadcast_to(ffn_norm_g.reshape(1, 1024), (128, 1024))
    c[:, C_G3:C_G3 + 1024] = np.broadcast_to(final_norm_g.reshape(1, 1024), (128, 1024))
    cw = conv_w.reshape(31, 4, 128)
    c[:, C_CW:C_CW + 124] = cw.transpose(2, 1, 0).reshape(128, 124)
    c[:, C_CB:C_CB + 4] = conv_b.reshape(4, 128).T
    c[:, C_LG:C_LG + 4] = conv_ln_g.reshape(4, 128).T
    c[:, C_LB:C_LB + 4] = conv_ln_b.reshape(4, 128).T
    c[:, C_B31:C_B31 + 8] = np.broadcast_to(rel_bias[31:32, :], (128, 8))
    i = np.arange(128)[:, None]
    j = np.arange(256)[None, :]
    dist = j - i
    idx = np.where(dist >= 0, _t5_bucket(np.maximum(dist, 0)), 32)
    ext = np.concatenate([rel_bias.astype(np.float32), np.full((1, 8), NEG, np.float32)], axis=0)
    tt = ext[idx]
    c[:, C_TT:C_TT + 2048] = tt.transpose(0, 2, 1).reshape(128, 2048)
    return c


_NC_CACHE = {}


def kernel(x, mix_norm_g, w_in, rel_bias, conv_w, conv_b, conv_ln_g, conv_ln_b,
           w_out, ffn_norm_g, w_gate, w_up, w_down, final_norm_g):
    f = lambda a: np.ascontiguousarray(np.asarray(a, dtype=np.float32))
    x = f(x)
    cst = _build_consts(f(mix_norm_g), f(rel_bias), f(conv_w), f(conv_b), f(conv_ln_g), f(conv_ln_b),
                        f(ffn_norm_g), f(final_norm_g))
    if "nc" not in _NC_CACHE:
        _NC_CACHE["nc"] = build_program()
    nc = _NC_CACHE["nc"]
    shared = {"w_in": f(w_in)[0], "w_out": f(w_out)[0], "w_gate": f(w_gate)[0], "w_up": f(w_up)[0],
              "w_down": f(w_down)[0], "cst": cst}
    in_maps = []
    for core in range(8):
        m = dict(shared)
        m["x"] = x[2 * core:2 * core + 2].reshape(T, D)
        in_maps.append(m)
    res = run_bass_kernel_spmd(nc, in_maps, core_ids=list(range(8)))
    outs = [np.asarray(r["out"], dtype=np.float32).reshape(2, SEQ, D) for r in res.results]
    return np.concatenate(outs, axis=0)
```

```python
import math
import numpy as np
import concourse.bass as bass
import concourse.mybir as mybir
from concourse.alu_op_type import AluOpType as ALU
from concourse.bass_utils import run_bass_kernel_spmd

F32 = mybir.dt.float32
BF16 = mybir.dt.bfloat16
AF = mybir.ActivationFunctionType
AX = mybir.AxisListType

D = 1024
T = 4096
SEQ = 2048
G = 512
NG = T // G
DFF = 2816
NFC = DFF // 128
INC = 2560
NEG = -30000.0
RMS_EPS = 1e-6
LN_EPS = 1e-5

C_G1, C_G2, C_G3 = 0, 1024, 2048
C_CW = 3072
C_CB = C_CW + 124
C_LG = C_CB + 4
C_LB = C_LG + 4
C_B31 = C_LB + 4
C_TT = C_B31 + 8
NCST = C_TT + 2048


import os
_STOP_AT = float(os.environ.get("KSTOP", "999"))


class _Stop(Exception):
    pass


def _chk(stage):
    if stage >= _STOP_AT:
        raise _Stop()


class Sched:
    CE = ("pe", "act", "dve", "pool")

    def __init__(self, nc, sems, sp_sems, pool_sems, act_sems):
        self.nc = nc
        self.sems = sems
        self.streams = {e: [] for e in ("pe", "act", "dve", "pool", "sp")}
        self.cnt = {e: 0 for e in self.CE}
        self.seen = {e: {} for e in self.streams}
        self.res = {}
        self.dma_sems = {"sp": sp_sems, "pool": pool_sems, "act": act_sems}
        self.dma_tot = {}
        self.dma_rr = {"sp": 0, "pool": 0, "act": 0}
        self.semobj = dict(sems)
        for q, lst in self.dma_sems.items():
            for i, s in enumerate(lst):
                self.semobj[(q, i)] = s
                self.dma_tot[(q, i)] = 0

    def _deps(self, eng, reads, writes):
        need = {}

        def add(ev):
            if ev is None:
                return
            k, v = ev
            if need.get(k, 0) < v:
                need[k] = v
        for r in reads:
            st = self.res.get(r)
            if st is not None:
                add(st["w"])
        for w in writes:
            st = self.res.get(w)
            if st is not None:
                if st["w"] is not None and (st["w"][0] != eng or eng != "pe"):
                    add(st["w"])
                for k, v in st["r"].items():
                    if k != eng or eng != "pe":
                        add((k, v))
        waits = []
        for k, v in need.items():
            if self.seen[eng].get(k, 0) < v:
                self.seen[eng][k] = v
                waits.append((self.semobj[k], v))
        return waits

    def _mark(self, ev, reads, writes):
        for r in reads:
            st = self.res.setdefault(r, {"w": None, "r": {}})
            if st["r"].get(ev[0], 0) < ev[1]:
                st["r"][ev[0]] = ev[1]
        for w in writes:
            self.res[w] = {"w": ev, "r": {}}

    def op(self, eng, fn, reads=(), writes=(), inc=True, tag=None):
        waits = self._deps(eng, reads, writes)
        if tag is not None and os.environ.get("KDBG"):
            print("DBG", tag, eng, [(str(s_), v) for s_, v in waits], "cnt", dict(self.cnt))
        if inc:
            self.cnt[eng] += 1
            ev = (eng, self.cnt[eng])
        else:
            ev = (eng, self.cnt[eng] + 1)
        self._mark(ev, reads, writes)
        sem = self.sems[eng]

        def emit(E, waits=waits, fn=fn, inc=inc, sem=sem):
            for s, v in waits:
                E.wait_ge(s, v)
            ins = fn(E)
            if inc:
                ins.then_inc(sem, 1)
        self.streams[eng].append(emit)

    def dma(self, q, out, in_, reads=(), writes=()):
        lst = self.dma_sems[q]
        i = self.dma_rr[q] % len(lst)
        self.dma_rr[q] += 1
        key = (q, i)
        waits = self._deps(q, reads, writes)
        prev = self.dma_tot[key]
        if prev > 0 and self.seen[q].get(key, 0) < prev:
            self.seen[q][key] = prev
            waits.append((self.semobj[key], prev))
        self.dma_tot[key] = prev + 16
        ev = (key, prev + 16)
        self._mark(ev, reads, writes)
        sem = self.semobj[key]

        def emit(E, waits=waits, out=out, in_=in_, sem=sem):
            for s, v in waits:
                E.wait_ge(s, v)
            E.dma_start(out=out, in_=in_).then_inc(sem, 16)
        self.streams[q].append(emit)

    def barrier(self, keep=(), skip_queue=None):
        for e in self.streams:
            waits = []
            for k in self.CE:
                v = self.cnt[k]
                if v > 0 and k != e and self.seen[e].get(k, 0) < v:
                    self.seen[e][k] = v
                    waits.append((self.semobj[k], v))
            for k, v in self.dma_tot.items():
                if k[0] == skip_queue:
                    continue
                if v > 0 and self.seen[e].get(k, 0) < v:
                    self.seen[e][k] = v
                    waits.append((self.semobj[k], v))

            def emit(E, waits=waits):
                for s, v in waits:
                    E.wait_ge(s, v)
            self.streams[e].append(emit)
        self.res = {k: v for k, v in self.res.items() if isinstance(k, tuple) and k[0] in keep}

    def finish_waits(self):
        waits = [(self.semobj[k], v) for k, v in self.dma_tot.items() if v > 0]
        waits += [(self.semobj[k], self.cnt[k]) for k in self.CE if self.cnt[k] > 0]

        def emit(E, waits=waits):
            for s, v in waits:
                E.wait_ge(s, v)
        self.streams["sp"].append(emit)


class Arena:
    def __init__(self, ap, nelem):
        self.ap = ap
        self.n = nelem
        self.off = 0
        self.hi = 0

    def mark(self):
        return self.off

    def reset(self, m):
        self.off = m

    def take(self, nbytes):
        nbytes = (nbytes + 63) // 64 * 64
        n16 = nbytes // 2
        o = self.off
        self.off += n16
        self.hi = max(self.hi, self.off)
        assert self.off <= self.n, f"arena overflow {self.off * 2} > {self.n * 2}"
        return self.ap[:, o:o + n16]

    def bf(self, *shape):
        n = int(np.prod(shape))
        v = self.take(n * 2)[:, 0:n]
        return self._shape(v, shape)

    def f32(self, *shape):
        n = int(np.prod(shape))
        v = self.take(n * 4).bitcast(F32)[:, 0:n]
        return self._shape(v, shape)

    @staticmethod
    def _shape(v, shape):
        if len(shape) == 1:
            return v
        if len(shape) == 2:
            return v.rearrange("p (a b) -> p a b", a=shape[0])
        if len(shape) == 3:
            return v.rearrange("p (a b c) -> p a b c", a=shape[0], b=shape[1])
        raise ValueError(shape)


ARENA_BYTES = 212480


def build_program():
    nc = bass.Bass("TRN2", target_bir_lowering=False)
    x = nc.dram_tensor("x", [T, D], F32, kind="ExternalInput").ap()
    w_in = nc.dram_tensor("w_in", [D, INC], F32, kind="ExternalInput").ap()
    w_out = nc.dram_tensor("w_out", [D, D], F32, kind="ExternalInput").ap()
    w_gate = nc.dram_tensor("w_gate", [D, DFF], F32, kind="ExternalInput").ap()
    w_up = nc.dram_tensor("w_up", [D, DFF], F32, kind="ExternalInput").ap()
    w_down = nc.dram_tensor("w_down", [DFF, D], F32, kind="ExternalInput").ap()
    cst = nc.dram_tensor("cst", [128, NCST], F32, kind="ExternalInput").ap()
    out = nc.dram_tensor("out", [T, D], F32, kind="ExternalOutput").ap()
    h1s = nc.dram_tensor("h1s", [T, D], F32, kind="Internal").ap()

    w_in_v = w_in.rearrange("(k p) n -> p k n", p=128)
    w_out_v = w_out.rearrange("(k p) n -> p k n", p=128)
    w_gate_v = w_gate.rearrange("(k p) n -> p k n", p=128)
    w_up_v = w_up.rearrange("(k p) n -> p k n", p=128)
    w_down_v = w_down.rearrange("(k p) n -> p k n", p=128)

    from contextlib import ExitStack
    with ExitStack() as es:
        arena_t = es.enter_context(nc.sbuf_tensor("arena", [128, ARENA_BYTES // 2], BF16))
        banks = [es.enter_context(nc.psum_tensor(f"bank{i}", [128, 512], F32)) for i in range(8)]
        sems = {e: es.enter_context(nc.semaphore(f"s_{e}")) for e in Sched.CE}
        sp_sems = [es.enter_context(nc.semaphore(f"d_sp{i}")) for i in range(8)]
        pool_sems = [es.enter_context(nc.semaphore(f"d_pl{i}")) for i in range(6)]
        act_sems = [es.enter_context(nc.semaphore(f"d_ac{i}")) for i in range(4)]
        block = es.enter_context(nc.Block())
        S = Sched(nc, sems, sp_sems, pool_sems, act_sems)
        A = Arena(arena_t[:, :], ARENA_BYTES // 2)

        ident = A.bf(128)
        ones_bf = A.bf(128)
        csm = A.f32(144)
        ssq = A.f32(64)
        csh = A.f32(144)
        rs = A.f32(64)
        CW = csm[:, 0:124]
        CB = csm[:, 124:128]
        LG = csm[:, 128:132]
        LB = csm[:, 132:136]
        B31 = csm[:, 136:144]
        CWh = csh[:, 0:124]
        LGh = csh[:, 128:132]
        LBh = csh[:, 132:136]
        m_persist = A.mark()

        diag = A.bf(124, 128)
        UC = A.bf(4, 30 + G)
        Y = A.f32(4, G)
        Yb = [A.bf(G) for _ in range(2)]
        Ysq = [A.bf(G) for _ in range(2)]
        K2 = A.bf(8, SEQ)
        VA = A.bf(16, 8, 128)
        w_out_bf = A.bf(8, 1024)
        Q2 = A.bf(8, G)
        g1b = A.f32(1024)
        xt = [A.f32(1024) for _ in range(2)]
        u_bf = [A.bf(1024) for _ in range(4)]
        uT = A.bf(8, G)
        wblk = [A.bf(8, 256) for _ in range(4)]
        NE = 4
        Eb = [A.bf(G) for _ in range(NE)]
        mixT = uT
        tmpf = [A.f32(G) for _ in range(3)]
        MU = A.f32(G)
        RSTD = A.f32(G)
        TThi = A.bf(8, 256)
        TTlo = A.bf(8, 256)
        Rb = [A.f32(G) for _ in range(1)]
        G8x = A.f32(4, 8, 8)
        TOPx = Rb[0][:, 0:256].rearrange("p (c h b) -> p c h b", c=4, h=8)
        NMx = A.bf(4, 64)
        KS = A.bf(8, 8)
        KSf = A.f32(8, 8)
        print("phase A arena bytes:", A.hi * 2)
        TTf = Y
        TTf3 = TTf.rearrange("p a b -> p (a b)").rearrange("p (h j) -> p h j", h=8)

        MMb = banks
        MMRING = (0, 1, 4, 5)
        misc = banks[2]
        TPS = banks[3][:, :].bitcast(BF16)
        Sb = [banks[6], banks[7]]
        SRING = [3, 2, 4, 5]
        OPHYS = [0, 1, 6, 7]
        Ob = [banks[i] for i in OPHYS]
        G_ps4 = misc[:, 0:256]
        G_ps4v = G_ps4.rearrange("p (c h b) -> p c h b", c=4, h=8)
        NMT4 = misc[:, 256:512].bitcast(BF16)

        S.dma("sp", csm[:, 0:144], cst[:, C_CW:C_CW + 144], writes=["csm"])
        S.dma("sp", g1b, cst[:, C_G1:C_G1 + 1024], writes=["g1b"])
        S.op("pool", lambda E: E.memset(ident, 0.0), writes=["ident"])
        S.op("pool", lambda E: E.affine_select(out=ident, in_=ident, pattern=[[-1, 128]], compare_op=ALU.not_equal,
                                               fill=1.0, base=0, channel_multiplier=1), reads=["ident"], writes=["ident"])
        S.op("pool", lambda E: E.memset(ones_bf, 1.0), writes=["ones"])
        S.op("pool", lambda E: E.memset(KSf, 0.0), writes=[("KSf", bb) for bb in range(8)])
        def late_setup():
            S.dma("sp", TTf.rearrange("p a b -> p (a b)"), cst[:, C_TT:C_TT + 2048], writes=["Y0", "Y1", "Y2", "Y3"])
            for kc in range(0, 8, 4):
                S.dma("pool", w_out_bf[:, kc:kc + 4, :], w_out_v[:, kc:kc + 4, :], writes=[("wo", kc // 4)])
            S.op("dve", lambda E: E.tensor_scalar(out=csh[:, 0:136], in0=csm[:, 0:136], scalar1=0.5, scalar2=None, op0=ALU.mult),
                 reads=["csm"], writes=["csh"])
            for cc in range(4):
                S.op("dve", lambda E, cc=cc: E.tensor_tensor(
                    out=diag[:, cc * 31:(cc + 1) * 31, :],
                    in0=ident.unsqueeze(1).to_broadcast([128, 31, 128]),
                    in1=CWh[:, cc * 31:(cc + 1) * 31].unsqueeze(2).to_broadcast([128, 31, 128]),
                    op=ALU.mult), reads=["ident", "csh"], writes=[("diag", cc)])
            S.op("dve", lambda E: E.tensor_tensor(out=TTf3, in0=TTf3, in1=B31.unsqueeze(2).to_broadcast([128, 8, 256]),
                                                  op=ALU.subtract), reads=["Y0", "Y1", "Y2", "Y3", "csm"], writes=["Y0", "Y1", "Y2", "Y3"])
            S.op("dve", lambda E: E.tensor_copy(out=TThi, in_=TTf3), reads=["Y0", "Y1", "Y2", "Y3"], writes=["TThi"])
            S.op("dve", lambda E: E.tensor_tensor(out=TTf3, in0=TTf3, in1=TThi, op=ALU.subtract),
                 reads=["Y0", "Y1", "Y2", "Y3", "TThi"], writes=["Y0", "Y1", "Y2", "Y3"])
            S.op("dve", lambda E: E.tensor_copy(out=TTlo, in_=TTf3), reads=["Y0", "Y1", "Y2", "Y3"], writes=["TTlo"])
            S.op("dve", lambda E: E.memset(VA[:, :, :, 64:128], 1.0), writes=["VAones"])
            S.op("dve", lambda E: E.memset(K2[64:128, :, :], 1.0), writes=[("K2i", h) for h in range(8)])
            for h in range(8):
                kv = K2[64:128, h, :].rearrange("p (b t) -> p b t", b=8)
                S.op("pool", lambda E, kv=kv, h=h: E.affine_select(out=kv, in_=kv, pattern=[[-1, 8], [0, 256]],
                                                                   compare_op=ALU.is_equal, fill=0.0, base=-8 * h,
                                                                   channel_multiplier=1),
                     reads=[("K2i", h)], writes=[("K2i", h)])

        cnt = {"xt": 0, "mm": 0, "wblk": 0, "tmpf": 0, "yb": 0, "g8": 0, "tp": 0}

        def rmsnorm_to_bf(src_tile, src_key, dst_bf, dst_key, gb, gb_key, col):
            S.op("act", lambda E: E.activation(out=dst_bf, in_=src_tile, func=AF.Square, accum_out=ssq[:, col:col + 1]),
                 reads=[src_key], writes=[dst_key, ("ssq", col)])
            S.op("act", lambda E: E.activation(out=rs[:, col:col + 1], in_=ssq[:, col:col + 1], func=AF.Ln,
                                               scale=1.0 / D, bias=RMS_EPS),
                 reads=[("ssq", col)], writes=[("rs", col)])
            S.op("act", lambda E: E.activation(out=rs[:, col:col + 1], in_=rs[:, col:col + 1], func=AF.Exp, scale=-0.5),
                 reads=[("rs", col)], writes=[("rs", col)])
            S.op("dve", lambda E: E.scalar_tensor_tensor(out=dst_bf, in0=src_tile, scalar=rs[:, col:col + 1], in1=gb,
                                                         op0=ALU.mult, op1=ALU.mult),
                 reads=[src_key, ("rs", col), gb_key], writes=[dst_key])

        TPSv = {i: banks[i][:, :].bitcast(BF16) for i in (3, 2, 6, 7)}

        def transposes(src_bf, src_key, dstT, dst_key, tc):
            bk = (3, 2, 6, 7)[cnt["tp"] % 4]
            cnt["tp"] += 1
            tps = TPSv[bk]
            for kc in range(8):
                S.op("pe", lambda E, kc=kc: E.transpose(tps[:, kc * 128:(kc + 1) * 128], src_bf[:, kc * 128:(kc + 1) * 128], ident),
                     reads=[src_key, "ident"], writes=[("pb", bk)], inc=(kc == 7))
            S.op("dve", lambda E: E.tensor_copy(out=dstT[:, :, tc * 128:(tc + 1) * 128],
                                                in_=tps.rearrange("p (k t) -> p k t", k=8)),
                 reads=[("pb", bk)], writes=[(dst_key, tc)] + [("mixT", kc, hf, tc) for kc in range(8) for hf in range(2)])

        def a1a_chunk(g, tc):
            c = 4 * g + tc
            b = cnt["xt"] % 2
            cnt["xt"] += 1
            S.dma("sp", xt[b], x[c * 128:(c + 1) * 128, :], writes=[("xt", b)])
            rmsnorm_to_bf(xt[b], ("xt", b), u_bf[tc], ("u_bf", tc), g1b, "g1b", tc)

        def a1a(g):
            for tc in range(4):
                a1a_chunk(g, tc)

        def a1b(g):
            for tc in range(4):
                transposes(u_bf[tc], ("u_bf", tc), uT, "uT", tc)

        BLK_ORDER = [("g", 0), ("a", 0), ("g", 1), ("a", 1), ("k", 0), ("k", 1), ("q", 0), ("q", 1), ("v", 0), ("v", 1)]
        BLK_COL = {"q": 0, "k": 512, "v": 1024, "a": 1536, "g": 2048}

        def issue_loads(upto):
            while cnt["wblk"] < min(upto, 10 * NG):
                idx = cnt["wblk"]
                cnt["wblk"] += 1
                name, half = BLK_ORDER[idx % 10]
                col0 = BLK_COL[name] + 256 * half
                wb = idx % 4
                S.dma("pool", wblk[wb], w_in_v[:, :, col0:col0 + 256], writes=[("wblk", wb)])

        def phase_a_part1(g):
            gi = g % 4
            t0 = gi * G
            uT_keys = [("uT", tc) for tc in range(4)]

            def use_blk(name, half, pair_with_prev=False):
                idx = 10 * g + BLK_ORDER.index((name, half))
                issue_loads((idx - 1 if pair_with_prev else idx) + 4)
                return idx % 4

            def fm_group(wb, j, per_chunk=False):
                mb = MMRING[cnt["mm"] % 4]
                cnt["mm"] += 1
                if per_chunk:
                    for tc in range(4):
                        for kc in range(8):
                            S.op("pe", lambda E, kc=kc, mb=mb, tc=tc: E.matmul(
                                MMb[mb][:, tc * 128:(tc + 1) * 128], lhsT=wblk[wb][:, kc, j * 128:(j + 1) * 128],
                                rhs=uT[:, kc, tc * 128:(tc + 1) * 128], start=(kc == 0), stop=(kc == 7), skip_group_check=True),
                                reads=[("wblk", wb), ("uT", tc)], writes=[("pb", mb)], inc=(kc == 7 and tc == 3))
                    return mb
                for kc in range(8):
                    S.op("pe", lambda E, kc=kc, mb=mb: E.matmul(MMb[mb][:, :], lhsT=wblk[wb][:, kc, j * 128:(j + 1) * 128],
                                                               rhs=uT[:, kc, :], start=(kc == 0), stop=(kc == 7),
                                                               skip_group_check=True),
                         reads=[("wblk", wb)] + uT_keys, writes=[("pb", mb)], inc=(kc == 7))
                return mb

            wb_g = [use_blk("g", 0), None]
            wb_a = [use_blk("a", 0, True), None]
            if gi == 0:
                S.op("pool", lambda E: E.memset(UC[:, :, 0:30], 0.0), writes=[("UCh", cc) for cc in range(4)])
            else:
                S.op("pool", lambda E: E.tensor_copy(out=UC[:, :, 0:30], in_=UC[:, :, G:G + 30]),
                     reads=[("UC", cc) for cc in range(4)], writes=[("UCh", cc) for cc in range(4)])
            sig_of = {}
            for cc in range(4):
                if cc == 2:
                    wb_g[1] = use_blk("g", 1)
                    wb_a[1] = use_blk("a", 1, True)
                mb = fm_group(wb_g[cc // 2], cc % 2, per_chunk=(cc == 0))
                r = cnt["tmpf"] % 3
                cnt["tmpf"] += 1
                S.op("act", lambda E, mb=mb, r=r: E.activation(out=tmpf[r], in_=MMb[mb][:, :], func=AF.Tanh, scale=0.5),
                     reads=[("pb", mb)], writes=[("tmpf", r)])
                mb2 = fm_group(wb_a[cc // 2], cc % 2)
                S.op("dve", lambda E, mb2=mb2, r=r, cc=cc: E.scalar_tensor_tensor(out=UC[:, cc, 30:30 + G], in0=tmpf[r], scalar=1.0,
                                                                                in1=MMb[mb2][:, :], op0=ALU.add, op1=ALU.mult),
                     reads=[("pb", mb2), ("tmpf", r)], writes=[("UC", cc)])
            if g + 1 < NG:
                a1a_chunk(g + 1, 0)
                a1a_chunk(g + 1, 1)
            for j in range(4):
                if j % 2 == 0:
                    wb_k = use_blk("k", j // 2)
                mb = fm_group(wb_k, j % 2)
                S.op("dve", lambda E, mb=mb, j=j: E.tensor_copy(out=K2[0:64, 2 * j, t0:t0 + G], in_=MMb[mb][0:64, :]),
                     reads=[("pb", mb)], writes=[("K2", 2 * j)])
                S.op("dve", lambda E, mb=mb, j=j: E.tensor_copy(out=K2[0:64, 2 * j + 1, t0:t0 + G], in_=MMb[mb][64:128, :]),
                     reads=[("pb", mb)], writes=[("K2", 2 * j + 1)])
            for bb in (2 * gi, 2 * gi + 1):
                S.op("dve", lambda E, bb=bb: E.tensor_reduce(out=KSf[0:64, :, bb:bb + 1],
                                                            in_=K2[0:64, :, bb * 256:(bb + 1) * 256], axis=AX.X, op=ALU.add),
                     reads=[("K2", h) for h in range(8)], writes=[("KSf", bb)])
            S.op("dve", lambda E: E.tensor_copy(out=KS[0:64, :, :], in_=KSf[0:64, :, :]),
                 reads=[("KSf", bb) for bb in range(8)], writes=[("KS", bb) for bb in range(8)])

            if g + 1 < NG:
                a1a_chunk(g + 1, 2)
            for j in range(4):
                if j % 2 == 0:
                    wb_q = use_blk("q", j // 2)
                mb = fm_group(wb_q, j % 2)
                S.op("act", lambda E, mb=mb, j=j: E.activation(out=Q2[0:64, 2 * j, :], in_=MMb[mb][0:64, :], func=AF.Copy, scale=0.125),
                     reads=[("pb", mb)], writes=[("Q2", 2 * j)])
                S.op("act", lambda E, mb=mb, j=j: E.activation(out=Q2[0:64, 2 * j + 1, :], in_=MMb[mb][64:128, :], func=AF.Copy, scale=0.125),
                     reads=[("pb", mb)], writes=[("Q2", 2 * j + 1)])
            if g + 1 < NG:
                a1a_chunk(g + 1, 3)
            wb_v = [use_blk("v", 0), use_blk("v", 1, True)]
            for tc in range(4):
                mb = MMRING[cnt["mm"] % 4]
                cnt["mm"] += 1
                cs = 4 * gi + tc
                for hv in range(2):
                    for kc in range(8):
                        S.op("pe", lambda E, kc=kc, mb=mb, tc=tc, hv=hv: E.matmul(
                            MMb[mb][:, hv * 256:(hv + 1) * 256], lhsT=uT[:, kc, tc * 128:(tc + 1) * 128],
                            rhs=wblk[wb_v[hv]][:, kc, :], start=(kc == 0), stop=(kc == 7), skip_group_check=True),
                            reads=[("wblk", wb_v[hv]), ("uT", tc)], writes=[("pb", mb)], inc=(kc == 7 and hv == 1))
                S.op("dve", lambda E, mb=mb, cs=cs: E.tensor_copy(out=VA[:, cs, :, 0:64],
                                                                 in_=MMb[mb][:, :].rearrange("p (h d) -> p h d", h=8)),
                     reads=[("pb", mb)], writes=[("VA", cs)])

            if g == 0:
                late_setup()
            _chk(2 + 10 * (g > 0))
            Q2m_keys = [("Q2m", hh) for hh in range(8)]
            R_keys = [("R", tq) for tq in range(4)]
            KS_keys = [("KS", bb) for bb in range(8)]
            if gi < 2:
                S.op("pool", lambda E: E.memset(Q2[64:128, :, :], 0.0), writes=Q2m_keys)
            else:
                owns = [(4 * gi + tc) // 2 for tc in range(4)]
                S.op("pool", lambda E: E.memset(G8x, -1e30), writes=["G8x"])
                for tc in range(4):
                    for h in range(8):
                        S.op("pe", lambda E, h=h, tc=tc, own=owns[tc]: E.matmul(
                            G_ps4[:, tc * 64 + h * 8:tc * 64 + h * 8 + own], lhsT=Q2[0:64, h, tc * 128:(tc + 1) * 128],
                            rhs=KS[0:64, h, 0:own], start=True, stop=True, skip_group_check=True),
                            reads=[("Q2", h)] + KS_keys, writes=[("pb", 2)], inc=(tc == 3 and h == 7))
                for pr in range(2):
                    S.op("dve", lambda E, pr=pr, own=owns[2 * pr]: E.tensor_copy(
                        out=G8x[:, 2 * pr:2 * pr + 2, :, 0:own], in_=G_ps4v[:, 2 * pr:2 * pr + 2, :, 0:own]),
                        reads=[("pb", 2), "G8x"], writes=["G8x"])
                for tc in range(4):
                    for h in range(8):
                        S.op("dve", lambda E, tc=tc, h=h: E.max(out=TOPx[:, tc, h, :], in_=G8x[:, tc, h, :]),
                             reads=["G8x"], writes=R_keys)
                S.op("dve", lambda E: E.tensor_tensor(out=G8x, in0=G8x, in1=TOPx[:, :, :, 2:3].to_broadcast([128, 4, 8, 8]),
                                                      op=ALU.is_lt), reads=["G8x"] + R_keys, writes=["G8x"])
                for pr in range(2):
                    S.op("dve", lambda E, pr=pr, own=owns[2 * pr]: E.memset(G8x[:, 2 * pr:2 * pr + 2, :, own:own + 1], 0.0),
                         reads=["G8x"], writes=["G8x"])
                S.op("dve", lambda E: E.tensor_scalar(out=NMx.rearrange("p c j -> p (c j)"), in0=G8x.rearrange("p c h b -> p (c h b)"),
                                                      scalar1=NEG, scalar2=None, op0=ALU.mult),
                     reads=["G8x"], writes=["NMx"])

            _chk(2.1 + 10 * (g > 0))
            def gate_b():
                if gi < 2:
                    return
                for tc in range(4):
                    S.op("pe", lambda E, tc=tc: E.transpose(NMT4[0:64, tc * 128:(tc + 1) * 128], NMx[:, tc, :], ident),
                         reads=["NMx", "ident"], writes=[("pb", 2)], inc=(tc == 3))
                for hh in range(8):
                    S.op("act", lambda E, hh=hh: E.activation(out=Q2[64:128, hh, :], in_=NMT4[0:64, :], func=AF.Copy),
                         reads=[("pb", 2)], writes=[("Q2m", hh)])


            pend_stats = []

            def emit_stats(cc, yb):
                S.op("pe", lambda E: E.matmul(Sb[0][:, :], lhsT=ones_bf, rhs=Yb[yb], start=(cc == 0), stop=(cc == 3),
                                              skip_group_check=True),
                     reads=["ones", ("Yb", yb)], writes=[("pb", 6)], inc=True)
                S.op("pe", lambda E: E.matmul(Sb[1][:, :], lhsT=ones_bf, rhs=Ysq[yb], start=(cc == 0), stop=(cc == 3),
                                              skip_group_check=True),
                     reads=["ones", ("Ysq", yb)], writes=[("pb", 7)], inc=True)
            for cc in range(4):
                mb = cc % 2
                for t in range(31):
                    S.op("pe", lambda E, t=t, mb=mb, cc=cc: E.matmul(MMb[mb][:, :], lhsT=diag[:, cc * 31 + t, :],
                                                                    rhs=UC[:, cc, t:t + G], start=(t == 0), stop=(t == 30),
                                                                    skip_group_check=True),
                         reads=[("diag", cc), ("UC", cc), ("UCh", cc)], writes=[("pb", mb)], inc=(t == 30),
                         tag=(("conv0", g) if (t == 0 and cc == 0) else None))
                yb = cnt["yb"] % 2
                cnt["yb"] += 1
                S.op("act", lambda E, mb=mb, cc=cc, yb=yb: E.activation(out=Yb[yb], in_=MMb[mb][:, :], func=AF.Identity,
                                                                       bias=CB[:, cc:cc + 1]),
                     reads=[("pb", mb), "csm"], writes=[("Yb", yb)])
                S.op("act", lambda E, mb=mb, cc=cc, yb=yb: E.activation(out=Ysq[yb], in_=MMb[mb][:, :], func=AF.Square,
                                                                       bias=CB[:, cc:cc + 1]),
                     reads=[("pb", mb), "csm"], writes=[("Ysq", yb)])
                S.op("act", lambda E, mb=mb, cc=cc: E.activation(out=Y[:, cc, :], in_=MMb[mb][:, :], func=AF.Identity,
                                                                bias=CB[:, cc:cc + 1]),
                     reads=[("pb", mb), "csm"], writes=[f"Y{cc}"])
                pend_stats.append((cc, yb))
                if len(pend_stats) > 1:
                    emit_stats(*pend_stats.pop(0))
                if cc == 2:
                    gate_b()
            emit_stats(*pend_stats.pop(0))
            _chk(2.2 + 10 * (g > 0))
            S.op("act", lambda E: E.activation(out=MU, in_=Sb[0][:, :], func=AF.Copy, scale=1.0 / 512), reads=[("pb", 6)], writes=["MU"])
            rq = cnt["tmpf"] % 3
            cnt["tmpf"] += 1
            S.op("dve", lambda E: E.tensor_tensor(out=tmpf[rq], in0=MU, in1=MU, op=ALU.mult), reads=["MU"], writes=[("tmpf", rq)])
            S.op("dve", lambda E: E.scalar_tensor_tensor(out=RSTD, in0=Sb[1][:, :], scalar=1.0 / 512, in1=tmpf[rq],
                                                         op0=ALU.mult, op1=ALU.subtract),
                 reads=[("pb", 7), ("tmpf", rq)], writes=["RSTD"])

        def ln_rstd(g):
            S.op("act", lambda E: E.activation(out=RSTD, in_=RSTD, func=AF.Ln, bias=LN_EPS), reads=["RSTD"], writes=["RSTD"])
            S.op("act", lambda E: E.activation(out=RSTD, in_=RSTD, func=AF.Exp, scale=-0.5), reads=["RSTD"], writes=["RSTD"])

        ln_slot = {}

        def ln_pre(g, cc):
            r = cnt["tmpf"] % 3
            cnt["tmpf"] += 1
            ln_slot[cc] = r
            S.op("dve", lambda E, cc=cc, r=r: E.tensor_tensor(out=tmpf[r], in0=Y[:, cc, :], in1=MU, op=ALU.subtract),
                 reads=[f"Y{cc}", "MU"], writes=[("tmpf", r)])
            S.op("pool", lambda E, r=r: E.tensor_tensor(out=tmpf[r], in0=tmpf[r], in1=RSTD, op=ALU.mult),
                 reads=[("tmpf", r), "RSTD"], writes=[("tmpf", r)])
            S.op("pool", lambda E, cc=cc, r=r: E.tensor_scalar(out=tmpf[r], in0=tmpf[r], scalar1=LGh[:, cc:cc + 1], scalar2=LBh[:, cc:cc + 1],
                                                             op0=ALU.mult, op1=ALU.add),
                 reads=[("tmpf", r), "csh"], writes=[("tmpf", r)])

        def ln_apply(g, ccs):
            uT_keys = [("uT", tc) for tc in range(4)]
            for cc in ccs:
                r = ln_slot[cc]
                S.op("act", lambda E, cc=cc, r=r: E.activation(out=Y[:, cc, :], in_=tmpf[r], func=AF.Tanh),
                     reads=[("tmpf", r)], writes=[f"Y{cc}"])
            for cc in ccs:
                r = ln_slot[cc]
                S.op("dve", lambda E, cc=cc, r=r: E.scalar_tensor_tensor(out=mixT[:, 4 + cc, :], in0=Y[:, cc, :], scalar=1.0, in1=tmpf[r],
                                                                       op0=ALU.add, op1=ALU.mult),
                     reads=[f"Y{cc}", ("tmpf", r)],
                     writes=[("mixT", 4 + cc, hf, tq) for hf in range(2) for tq in range(4)] + uT_keys)

        def phase_a_part2(g):
            gi = g % 4
            uT_keys = [("uT", tc) for tc in range(4)]
            _chk(4 + 10 * (g > 0))
            steps = []
            for h in range(8):
                first_kc = 4 * gi
                order = [first_kc] + list(range(0, first_kc)) + [first_kc + 1, first_kc + 2, first_kc + 3]
                for i, kc in enumerate(order):
                    j = kc - first_kc
                    q0 = max(j, 0) * 128
                    tadd = None
                    if j >= 0:
                        ncols = 256 if j < 3 else 128
                        tadd = (0, ncols, q0)
                    elif j == -1:
                        tadd = (128, 128, 0)
                    steps.append(dict(h=h, kc=kc, q0=q0, tadd=tadd, first=(i == 0), last=(i == len(order) - 1), idx=len(steps)))

            def emit_qk(st):
                h, kc, q0, sb = st["h"], st["kc"], st["q0"], st["idx"] % len(SRING)
                tadd = st["tadd"]
                S.op("pe", lambda E: E.matmul(banks[SRING[sb]][:, q0:G], lhsT=K2[:, h, kc * 128:(kc + 1) * 128], rhs=Q2[:, h, q0:G],
                                              start=True, stop=(tadd is None), skip_group_check=True),
                     reads=[("K2", h), ("K2i", h), ("Q2", h), ("Q2m", h)], writes=[("pb", SRING[sb])],
                     inc=(tadd is None))
                if tadd is not None:
                    c0, n, s0 = tadd
                    S.op("pe", lambda E: E.matmul(banks[SRING[sb]][:, s0:s0 + n], lhsT=ident, rhs=TThi[:, h, c0:c0 + n],
                                                  start=False, stop=False, skip_group_check=True),
                         reads=["ident", "TThi"], writes=[("pb", SRING[sb])], inc=False)
                    S.op("pe", lambda E: E.matmul(banks[SRING[sb]][:, s0:s0 + n], lhsT=ident, rhs=TTlo[:, h, c0:c0 + n],
                                                  start=False, stop=True, skip_group_check=True),
                         reads=["ident", "TTlo"], writes=[("pb", SRING[sb])], inc=True)

            def emit_exp(st):
                q0, sb, e = st["q0"], st["idx"] % len(SRING), st["idx"] % NE
                S.op("act", lambda E: E.activation(out=Eb[e][:, q0:G], in_=banks[SRING[sb]][:, q0:G], func=AF.Exp),
                     reads=[("pb", SRING[sb])], writes=[("E", e)])

            def emit_pv(st):
                h, kc, q0, e, ob = st["h"], st["kc"], st["q0"], st["idx"] % NE, st["h"] % 4
                cs = kc
                S.op("pe", lambda E: E.matmul(Ob[ob][:, q0:G], lhsT=VA[:, cs, h, :], rhs=Eb[e][:, q0:G],
                                              start=st["first"], stop=st["last"], skip_group_check=True),
                     reads=[("VA", cs), "VAones", ("E", e)], writes=[("pb", OPHYS[ob])], inc=True)
                if st["last"]:
                    p0 = (h % 2) * 64
                    pieces = [(tq * 128, (tq + 1) * 128, [tq]) for tq in range(4)] if h == 7 else [(0, G, [0, 1, 2, 3])]
                    use_act = (gi == 0) and (h in (1, 2, 4, 5, 7))
                    for (a, bnd, tqs) in pieces:
                        if use_act:
                            S.op("act", lambda E, a=a, bnd=bnd: E.activation(out=Rb[0][0:64, a:bnd], in_=Ob[ob][64:128, a:bnd], func=AF.Ln),
                                 reads=[("pb", OPHYS[ob])], writes=[("R", tq) for tq in tqs])
                            S.op("act", lambda E, a=a, bnd=bnd: E.activation(out=Rb[0][0:64, a:bnd], in_=Rb[0][0:64, a:bnd], func=AF.Exp,
                                                                            scale=-1.0),
                                 reads=[("R", tq) for tq in tqs], writes=[("R", tq) for tq in tqs])
                        else:
                            S.op("dve", lambda E, a=a, bnd=bnd: E.reciprocal(out=Rb[0][0:64, a:bnd], in_=Ob[ob][64:128, a:bnd]),
                                 reads=[("pb", OPHYS[ob])], writes=[("R", tq) for tq in tqs])
                        S.op("dve", lambda E, a=a, bnd=bnd: E.tensor_tensor(out=mixT[p0:p0 + 64, h // 2, a:bnd], in0=Ob[ob][0:64, a:bnd],
                                                                          in1=Rb[0][0:64, a:bnd], op=ALU.mult),
                             reads=[("pb", OPHYS[ob])] + [("R", tq) for tq in tqs],
                             writes=[("mixT", h // 2, h % 2, tq) for tq in tqs] + [("uT", tq) for tq in tqs])

            LA = 3
            pend = []
            preload = {}
            xpre = {}
            for st in steps:
                emit_qk(st)
                emit_exp(st)
                pend.append(st)
                if len(pend) > LA:
                    emit_pv(pend.pop(0))
                if st["last"]:
                    h = st["h"]
                    if h == 0:
                        ln_rstd(g)
                    if h == 4:
                        ln_pre(g, 0)
                        ln_pre(g, 1)
                        ln_pre(g, 2)
                    if h == 5:
                        ln_apply(g, (0, 1, 2))
                        ln_pre(g, 3)
                    if h == 6:
                        ln_apply(g, (3,))
                    if h == 6:
                        for tq in range(2):
                            b = cnt["xt"] % 2
                            cnt["xt"] += 1
                            S.dma("sp", xt[b], x[(4 * g + tq) * 128:(4 * g + tq + 1) * 128, :], writes=[("xt", b)])
                            preload[tq] = b
            while pend:
                emit_pv(pend.pop(0))

            _chk(5 + 10 * (g > 0))
            for tc in range(4):
                c = 4 * g + tc
                mix_keys = [("mixT", kc, hf, tc) for kc in range(8) for hf in range(2)]
                if tc in preload:
                    b = preload[tc]
                else:
                    b = cnt["xt"] % 2
                    cnt["xt"] += 1
                    S.dma("sp", xt[b], x[c * 128:(c + 1) * 128, :], writes=[("xt", b)])
                for half in range(2):
                    mb = (0, 1, 4, 5)[(2 * tc + half) % 4]
                    for kc in range(8):
                        S.op("pe", lambda E, kc=kc, mb=mb, tc=tc, half=half: E.matmul(
                            banks[mb][:, :], lhsT=mixT[:, kc, tc * 128:(tc + 1) * 128], rhs=w_out_bf[:, kc, half * 512:(half + 1) * 512],
                            start=(kc == 0), stop=(kc == 7), skip_group_check=True),
                            reads=[("mixT", kc, 0, tc), ("mixT", kc, 1, tc), ("wo", kc // 4)],
                            writes=[("pb", mb)], inc=(kc == 7))
                    S.op("dve", lambda E, mb=mb, b=b, half=half: E.tensor_tensor(
                        out=xt[b][:, half * 512:(half + 1) * 512], in0=banks[mb][:, :], in1=xt[b][:, half * 512:(half + 1) * 512], op=ALU.add),
                        reads=[("pb", mb), ("xt", b)], writes=[("xt", b)])
                S.dma("act", h1s[c * 128:(c + 1) * 128, :], xt[b], reads=[("xt", b)], writes=[("h1s", c)])

        try:
          _chk(0)
          a1a(0)
          a1b(0)
          for g in range(NG):
            phase_a_part1(g)
            phase_a_part2(g)
            if g + 1 < NG:
                a1b(g + 1)
          _chk(20)
        except _Stop:
          pass

        m_end_a = A.mark()
        A.reset(m_persist)
        wu_bf = A.bf(8, DFF)
        wg_bf = A.bf(8, DFF)
        colblocks = [(i * 512, min(512, DFF - i * 512)) for i in range((DFF + 511) // 512)]
        dead_u = [("diag", cc) for cc in range(4)] + [("UC", cc) for cc in range(4)] + [("UCh", cc) for cc in range(4)] \
            + [f"Y{cc}" for cc in range(4)] + [("Yb", i) for i in range(2)] + [("Ysq", i) for i in range(2)]
        dead_g = [("Yb", i) for i in range(2)] + [("Ysq", i) for i in range(2)] + [("K2", h) for h in range(8)] \
            + [("K2i", h) for h in range(8)] + [("VA", cs) for cs in range(16)] + ["VAones"]

        def load_w(kind, bi, extra=()):
            c0, n = colblocks[bi]
            dst, src = (wg_bf, w_gate_v) if kind == "wg" else (wu_bf, w_up_v)
            for kc in range(0, 8, 4):
                S.dma("pool", dst[:, kc:kc + 4, c0:c0 + n], src[:, kc:kc + 4, c0:c0 + n],
                      writes=[(kind, bi, kc // 4)] + list(extra))
        PRE_U, PRE_G = 3, 2
        if _STOP_AT > 900:
            for bi in range(PRE_U):
                load_w("wu", bi, dead_u)
            for bi in range(PRE_G):
                load_w("wg", bi, dead_g)
        S.barrier(keep=("wg", "wu"), skip_queue="pool")
        wd_bf = A.bf(NFC, 1024)
        g2b = A.f32(1024)
        g3b = A.f32(1024)
        ht = [A.f32(1024) for _ in range(2)]
        hb = [A.f32(1024) for _ in range(2)]
        u2_bf = [A.bf(1024) for _ in range(4)]
        u2Tb = [A.bf(8, G) for _ in range(2)]
        hT = A.bf(NFC, G)
        SG = [A.bf(G) for _ in range(2)]
        print("phase B arena bytes:", A.hi * 2, "now", A.off * 2)

        pre_done = (_STOP_AT > 900)
        for bi in range(len(colblocks)):
            if not (pre_done and bi < PRE_G):
                load_w("wg", bi)
            if not (pre_done and bi < PRE_U):
                load_w("wu", bi)
        dchunks = [(0, 6), (6, 6), (12, 5), (17, 5)]
        for di, (f0, nf) in enumerate(dchunks):
            S.dma("pool", wd_bf[:, f0:f0 + nf, :], w_down_v[:, f0:f0 + nf, :], writes=[("wd", di)])

        def wd_key(fc):
            for di, (f0, nf) in enumerate(dchunks):
                if f0 <= fc < f0 + nf:
                    return ("wd", di)

        PG = [banks[0], banks[1]]
        PU = [banks[2], banks[3]]
        PD = [banks[4], banks[5]]
        TPSB = banks[6][:, :].bitcast(BF16)
        cntb = {"ht": 0, "hb": 0, "p": 0, "pd": 0, "sg": 0, "u2": 0, "tp": 0}

        TPSBv = {i: banks[i][:, :].bitcast(BF16) for i in (6, 7, 0, 2)}

        def transposes_b(src_bf, src_key, dstT, dst_key, tc):
            bk = (6, 7, 0, 2)[cntb["tp"] % 4]
            cntb["tp"] += 1
            tps = TPSBv[bk]
            for kc in range(8):
                S.op("pe", lambda E, kc=kc: E.transpose(tps[:, kc * 128:(kc + 1) * 128], src_bf[:, kc * 128:(kc + 1) * 128], ident),
                     reads=[src_key, "identB"], writes=[("pbB", bk)], inc=(kc == 7))
            S.op("dve", lambda E: E.tensor_copy(out=dstT[:, :, tc * 128:(tc + 1) * 128],
                                                in_=tps.rearrange("p (k t) -> p k t", k=8)),
                 reads=[("pbB", bk)], writes=[(dst_key, tc)])

        def b1a_chunk(g, tc):
            c = 4 * g + tc
            b = cntb["hb"] % 2
            cntb["hb"] += 1
            S.dma("sp", hb[b], h1s[c * 128:(c + 1) * 128, :], reads=[("h1s", c)], writes=[("hb", b)])
            rmsnorm_to_bf(hb[b], ("hb", b), u2_bf[tc], ("u2_bf", tc), g2b, "g2b", 8 + b)

        def b1a(g):
            for tc in range(4):
                b1a_chunk(g, tc)

        def b1a0():
            slots = ((hb[0], ("hb", 0), 8), (hb[1], ("hb", 1), 9), (ht[0], ("ht", 0), 10), (ht[1], ("ht", 1), 11))
            for tc in range(4):
                buf, key, col = slots[tc]
                S.dma("sp", buf, h1s[tc * 128:(tc + 1) * 128, :], reads=[("h1s", tc)], writes=[key])
            S.dma("sp", g2b, cst[:, C_G2:C_G2 + 1024], writes=["g2b"])
            S.dma("sp", g3b, cst[:, C_G3:C_G3 + 1024], writes=["g3b"])
            for tc in range(4):
                buf, key, col = slots[tc]
                rmsnorm_to_bf(buf, key, u2_bf[tc], ("u2_bf", tc), g2b, "g2b", col)

        def b1b(g):
            for tc in range(4):
                transposes_b(u2_bf[tc], ("u2_bf", tc), u2Tb[g % 2], ("u2T", g % 2), tc)

        def b2(g):
            u2T = u2Tb[g % 2]
            u2T_keys = [(("u2T", g % 2), tc) for tc in range(4)]
            for fc in range(NFC):
                bi = (fc * 128) // 512
                pb = cntb["p"] % 2
                cntb["p"] += 1
                for kc in range(8):
                    S.op("pe", lambda E, kc=kc, pb=pb, fc=fc: E.matmul(PG[pb][:, :], lhsT=wg_bf[:, kc, fc * 128:(fc + 1) * 128],
                                                                      rhs=u2T[:, kc, :], start=(kc == 0), stop=(kc == 7),
                                                                      skip_group_check=True),
                         reads=[("wg", bi, kc // 4)] + u2T_keys, writes=[("pbB", pb)], inc=(kc == 7))
                for kc in range(8):
                    S.op("pe", lambda E, kc=kc, pb=pb, fc=fc: E.matmul(PU[pb][:, :], lhsT=wu_bf[:, kc, fc * 128:(fc + 1) * 128],
                                                                      rhs=u2T[:, kc, :], start=(kc == 0), stop=(kc == 7),
                                                                      skip_group_check=True),
                         reads=[("wu", bi, kc // 4)] + u2T_keys, writes=[("pbB", 2 + pb)], inc=(kc == 7))
                sg = cntb["sg"] % 2
                cntb["sg"] += 1
                S.op("act", lambda E, pb=pb, sg=sg: E.activation(out=SG[sg], in_=PG[pb][:, :], func=AF.Silu),
                     reads=[("pbB", pb)], writes=[("SG", sg)])
                S.op("dve", lambda E, pb=pb, sg=sg, fc=fc: E.tensor_tensor(out=hT[:, fc, :], in0=PU[pb][:, :], in1=SG[sg], op=ALU.mult),
                     reads=[("pbB", 2 + pb), ("SG", sg)], writes=[("hT", fc)])
                if g + 1 < NG and fc % 4 == 1 and fc // 4 < 4:
                    b1a_chunk(g + 1, fc // 4)

        def b3(g):
            for tc in range(4):
                c = 4 * g + tc
                b = cntb["ht"] % 2
                cntb["ht"] += 1
                S.dma("sp", ht[b], h1s[c * 128:(c + 1) * 128, :], reads=[("h1s", c)], writes=[("ht", b)])
                for half in range(2):
                    pd = cntb["pd"] % 2
                    cntb["pd"] += 1
                    for fc in range(NFC):
                        S.op("pe", lambda E, fc=fc, pd=pd, tc=tc, half=half: E.matmul(
                            PD[pd][:, :], lhsT=hT[:, fc, tc * 128:(tc + 1) * 128], rhs=wd_bf[:, fc, half * 512:(half + 1) * 512],
                            start=(fc == 0), stop=(fc == NFC - 1), skip_group_check=True),
                            reads=[("hT", fc), wd_key(fc)], writes=[("pbB", 4 + pd)], inc=(fc == NFC - 1))
                    S.op("dve", lambda E, pd=pd, b=b, half=half: E.tensor_tensor(
                        out=ht[b][:, half * 512:(half + 1) * 512], in0=PD[pd][:, :], in1=ht[b][:, half * 512:(half + 1) * 512], op=ALU.add),
                        reads=[("pbB", 4 + pd), ("ht", b)], writes=[("ht", b)])
                col = 16 + b
                jk = (tc + 2) % 4
                S.op("act", lambda E, b=b, col=col, jk=jk: E.activation(out=u2_bf[jk], in_=ht[b], func=AF.Square, accum_out=ssq[:, col:col + 1]),
                     reads=[("ht", b)], writes=[("u2_bf", jk), ("ssq", col)])
                S.op("act", lambda E, col=col: E.activation(out=rs[:, col:col + 1], in_=ssq[:, col:col + 1], func=AF.Ln,
                                                           scale=1.0 / D, bias=RMS_EPS),
                     reads=[("ssq", col)], writes=[("rs", col)])
                S.op("act", lambda E, col=col: E.activation(out=rs[:, col:col + 1], in_=rs[:, col:col + 1], func=AF.Exp, scale=-0.5),
                     reads=[("rs", col)], writes=[("rs", col)])
                S.op("dve", lambda E, b=b, col=col: E.scalar_tensor_tensor(out=ht[b], in0=ht[b], scalar=rs[:, col:col + 1], in1=g3b,
                                                                          op0=ALU.mult, op1=ALU.mult),
                     reads=[("ht", b), ("rs", col), "g3b"], writes=[("ht", b)])
                S.dma("act", out[c * 128:(c + 1) * 128, :], ht[b], reads=[("ht", b)], writes=[("out", c)])

        try:
          _chk(21)
          b1a0()
          b1b(0)
          for g in range(NG):
            b2(g)
            if g + 1 < NG:
                b1b(g + 1)
            b3(g)
        except _Stop:
          pass
        S.finish_waits()

        def run(stream):
            def f(E):
                for emit in stream:
                    emit(E)
            return f
        block.sync(run(S.streams["sp"]))
        block.gpsimd(run(S.streams["pool"]))
        block.scalar(run(S.streams["act"]))
        block.vector(run(S.streams["dve"]))
        block.tensor(run(S.streams["pe"]))
        print("instr counts:", {k: len(v) for k, v in S.streams.items()}, "sem counts:", S.cnt)
    return nc


def _t5_bucket(d):
    max_exact = 16
    dd = np.maximum(d, 1).astype(np.float32)
    large = max_exact + (np.log(dd / np.float32(max_exact)) / np.float32(math.log(128 / max_exact))
                         * np.float32(32 - max_exact)).astype(np.int32)
    large = np.minimum(large, 31)
    return np.where(d < max_exact, d, large)


def _build_consts(mix_norm_g, rel_bias, conv_w, conv_b, conv_ln_g, conv_ln_b, ffn_norm_g, final_norm_g):
    c = np.zeros((128, NCST), np.float32)
    c[:, C_G1:C_G1 + 1024] = np.broadcast_to(mix_norm_g.reshape(1, 1024), (128, 1024))
    c[:, C_G2:C_G2 + 1024] = np.broadcast_to(ffn_norm_g.reshape(1, 1024), (128, 1024))
    c[:, C_G3:C_G3 + 1024] = np.broadcast_to(final_norm_g.reshape(1, 1024), (128, 1024))
    cw = conv_w.reshape(31, 4, 128)
    c[:, C_CW:C_CW + 124] = cw.transpose(2, 1, 0).reshape(128, 124)
    c[:, C_CB:C_CB + 4] = conv_b.reshape(4, 128).T
    c[:, C_LG:C_LG + 4] = conv_ln_g.reshape(4, 128).T
    c[:, C_LB:C_LB + 4] = conv_ln_b.reshape(4, 128).T
    c[:, C_B31:C_B31 + 8] = np.broadcast_to(rel_bias[31:32, :], (128, 8))
    i = np.arange(128)[:, None]
    j = np.arange(256)[None, :]
    dist = j - i
    idx = np.where(dist >= 0, _t5_bucket(np.maximum(dist, 0)), 32)
    ext = np.concatenate([rel_bias.astype(np.float32), np.full((1, 8), NEG, np.float32)], axis=0)
    tt = ext[idx]
    c[:, C_TT:C_TT + 2048] = tt.transpose(0, 2, 1).reshape(128, 2048)
    return c


_NC_CACHE = {}


def kernel(x, mix_norm_g, w_in, rel_bias, conv_w, conv_b, conv_ln_g, conv_ln_b,
           w_out, ffn_norm_g, w_gate, w_up, w_down, final_norm_g):
    f = lambda a: np.ascontiguousarray(np.asarray(a, dtype=np.float32))
    x = f(x)
    cst = _build_consts(f(mix_norm_g), f(rel_bias), f(conv_w), f(conv_b), f(conv_ln_g), f(conv_ln_b),
                        f(ffn_norm_g), f(final_norm_g))
    if "nc" not in _NC_CACHE:
        _NC_CACHE["nc"] = build_program()
    nc = _NC_CACHE["nc"]
    shared = {"w_in": f(w_in)[0], "w_out": f(w_out)[0], "w_gate": f(w_gate)[0], "w_up": f(w_up)[0],
              "w_down": f(w_down)[0], "cst": cst}
    in_maps = []
    for core in range(8):
        m = dict(shared)
        m["x"] = x[2 * core:2 * core + 2].reshape(T, D)
        in_maps.append(m)
    res = run_bass_kernel_spmd(nc, in_maps, core_ids=list(range(8)))
    outs = [np.asarray(r["out"], dtype=np.float32).reshape(2, SEQ, D) for r in res.results]
    return np.concatenate(outs, axis=0)
```
